# Optimizing a Trainium2 kernel written in Bass

```python
import jax, jax.numpy as jnp
from jax import lax
import numpy as np

D_MODEL = 2048
BATCH = 1
SEQ = 8192
DEPTH = 1

CTX_LEN = 256
GRID_W = 64
HEAD_DIM = 64
N_Q_HEADS = D_MODEL // (2 * HEAD_DIM)
N_KV_HEADS = N_Q_HEADS // 4
ATTN_WIDTH = N_Q_HEADS * HEAD_DIM
KV_WIDTH = N_KV_HEADS * HEAD_DIM
POOL_WINDOWS = (2, 4, 8, 16)
POOL_GROUPS = len(POOL_WINDOWS)
POOL_WIDTH = D_MODEL - ATTN_WIDTH
POOL_GROUP_DIM = POOL_WIDTH // POOL_GROUPS
MIX_WIDTH = ATTN_WIDTH + POOL_WIDTH
IN_WIDTH = ATTN_WIDTH + 2 * KV_WIDTH + POOL_WIDTH
D_FF = 4 * D_MODEL
WINDOW = 128
BLOCK = 128
ROPE_BASE = 10000.0
N_MOD = 6
EPS = 1e-6
NEG_INF = -1e30

kernel_name = "hymba_style_window_gqa_pool_diffusion_block"


def rms_norm(x, w):
    xf = x.astype(jnp.float32)
    y = xf * lax.rsqrt(jnp.mean(xf * xf, axis=-1, keepdims=True) + EPS)
    return (y * w.astype(jnp.float32)).astype(x.dtype)


def ada_modulation(cond, w, b):
    m = jax.nn.silu(cond) @ w + b
    return jnp.split(m[..., None, :], N_MOD, axis=-1)


def modulate(h, shift, scale):
    return h * (1.0 + scale) + shift


def split_projection(p):
    B, L, _ = p.shape
    q, k, v, u = jnp.split(p, [ATTN_WIDTH, ATTN_WIDTH + KV_WIDTH, ATTN_WIDTH + 2 * KV_WIDTH], axis=-1)
    return (q.reshape(B, L, N_Q_HEADS, HEAD_DIM), k.reshape(B, L, N_KV_HEADS, HEAD_DIM),
            v.reshape(B, L, N_KV_HEADS, HEAD_DIM), u)


def axial_positions(seq_len):
    rows = seq_len // GRID_W
    row = jnp.broadcast_to(jnp.arange(rows)[:, None], (rows, GRID_W)).reshape(-1)
    col = jnp.broadcast_to(jnp.arange(GRID_W)[None, :], (rows, GRID_W)).reshape(-1)
    return row, col


def rope_2d(x, row, col):
    half = HEAD_DIM // 2
    inv_freq = ROPE_BASE ** (-jnp.arange(0, half, 2, dtype=jnp.float32) / half)

    def rot(xa, pos):
        ang = pos.astype(jnp.float32)[:, None] * inv_freq[None, :]
        cos = jnp.cos(ang)[None, :, None, :]
        sin = jnp.sin(ang)[None, :, None, :]
        x1, x2 = jnp.split(xa, 2, axis=-1)
        return jnp.concatenate([x1 * cos - x2 * sin, x1 * sin + x2 * cos], axis=-1)

    xf = x.astype(jnp.float32)
    out = jnp.concatenate([rot(xf[..., :half], row), rot(xf[..., half:], col)], axis=-1)
    return out.astype(x.dtype)


def latent_window_attention(q, k, v, k_ctx, v_ctx, sink):
    B, L, H, D = q.shape
    G = H // N_KV_HEADS
    nb = L // BLOCK
    scale = HEAD_DIM ** -0.5
    qb = q.reshape(B, nb, BLOCK, N_KV_HEADS, G, D)
    pad = ((0, 0), (BLOCK, BLOCK), (0, 0), (0, 0))
    kp = jnp.pad(k, pad).reshape(B, nb + 2, BLOCK, N_KV_HEADS, D)
    vp = jnp.pad(v, pad).reshape(B, nb + 2, BLOCK, N_KV_HEADS, D)
    kb = jnp.concatenate([kp[:, :-2], kp[:, 1:-1], kp[:, 2:]], axis=2)
    vb = jnp.concatenate([vp[:, :-2], vp[:, 1:-1], vp[:, 2:]], axis=2)
    s_win = jnp.einsum('bnqhgd,bnkhd->bhgnqk', qb, kb).astype(jnp.float32) * scale
    blk = jnp.arange(nb)[:, None, None] * BLOCK
    qpos = blk + jnp.arange(BLOCK)[None, :, None]
    kpos = blk - BLOCK + jnp.arange(3 * BLOCK)[None, None, :]
    valid = (jnp.abs(qpos - kpos) <= WINDOW) & (kpos >= 0) & (kpos < L)
    s_win = jnp.where(valid, s_win, NEG_INF)
    s_ctx = jnp.einsum('bnqhgd,bchd->bhgnqc', qb, k_ctx).astype(jnp.float32) * scale
    s_sink = jnp.broadcast_to(sink.astype(jnp.float32).reshape(N_KV_HEADS, G)[None, :, :, None, None, None],
                              s_win.shape[:-1] + (1,))
    p = jax.nn.softmax(jnp.concatenate([s_win, s_ctx, s_sink], axis=-1), axis=-1)
    n_win = 3 * BLOCK
    n_ctx = k_ctx.shape[1]
    p_win = p[..., :n_win].astype(v.dtype)
    p_ctx = p[..., n_win:n_win + n_ctx].astype(v.dtype)
    out = (jnp.einsum('bhgnqk,bnkhd->bnqhgd', p_win, vb)
           + jnp.einsum('bhgnqc,bchd->bnqhgd', p_ctx, v_ctx))
    return out.reshape(B, L, H * D)


def context_attention(q, k, v, sink):
    B, C, H, D = q.shape
    G = H // N_KV_HEADS
    qg = q.reshape(B, C, N_KV_HEADS, G, D)
    s = jnp.einsum('bqhgd,bkhd->bhgqk', qg, k).astype(jnp.float32) * (HEAD_DIM ** -0.5)
    s_sink = jnp.broadcast_to(sink.astype(jnp.float32).reshape(N_KV_HEADS, G)[None, :, :, None, None],
                              s.shape[:-1] + (1,))
    p = jax.nn.softmax(jnp.concatenate([s, s_sink], axis=-1), axis=-1)[..., :C].astype(v.dtype)
    out = jnp.einsum('bhgqk,bkhd->bqhgd', p, v)
    return out.reshape(B, C, H * D)


def multiscale_pool(u, pool_w, pool_scale):
    B, L, _ = u.shape
    uf = u.astype(jnp.float32)
    csum = jnp.pad(jnp.cumsum(uf, axis=1), ((0, 0), (1, 0), (0, 0)))
    t = jnp.arange(L)
    outs = []
    for g, w in enumerate(POOL_WINDOWS):
        lo = jnp.clip(t - w // 2, 0, L)
        hi = jnp.clip(t - w // 2 + w, 0, L)
        cs = csum[..., g * POOL_GROUP_DIM:(g + 1) * POOL_GROUP_DIM]
        mean = (cs[:, hi] - cs[:, lo]) / (hi - lo).astype(jnp.float32)[None, :, None]
        outs.append(mean - uf[..., g * POOL_GROUP_DIM:(g + 1) * POOL_GROUP_DIM])
    pooled = jnp.stack(outs, axis=2)
    mixed = jnp.einsum('blgc,gcd->blgd', pooled, pool_w.astype(jnp.float32)).reshape(B, L, POOL_WIDTH)
    return (mixed * pool_scale.astype(jnp.float32)).astype(u.dtype)


def squared_relu_mlp(h, w_up, w_down):
    return jnp.square(jax.nn.relu(h @ w_up)) @ w_down


def setup_inputs(seed: int = 0) -> dict:
    key = jax.random.key(seed)
    ks = jax.random.split(key, 18)
    f32 = jnp.float32
    nrm = lambda k, shape, s: jax.random.normal(k, shape, f32) * s
    return {
        "x": nrm(ks[0], (BATCH, SEQ, D_MODEL), 1.0),
        "c": nrm(ks[1], (BATCH, D_MODEL), 1.0),
        "ctx": nrm(ks[2], (BATCH, CTX_LEN, D_MODEL), 1.0),
        "c_ctx": nrm(ks[3], (D_MODEL,), 1.0),
        "norm_attn_w": 1.0 + nrm(ks[4], (DEPTH, D_MODEL), 0.02),
        "norm_mlp_w": 1.0 + nrm(ks[5], (DEPTH, D_MODEL), 0.02),
        "w_ada": nrm(ks[6], (DEPTH, D_MODEL, N_MOD * D_MODEL), 0.5 * D_MODEL ** -0.5),
        "b_ada": nrm(ks[7], (DEPTH, N_MOD * D_MODEL), 0.02),
        "w_in": nrm(ks[8], (DEPTH, D_MODEL, IN_WIDTH), D_MODEL ** -0.5),
        "attn_sink": nrm(ks[9], (DEPTH, N_Q_HEADS), 1.0),
        "pool_w": nrm(ks[10], (DEPTH, POOL_GROUPS, POOL_GROUP_DIM, POOL_GROUP_DIM), POOL_GROUP_DIM ** -0.5),
        "pool_scale": 1.0 + nrm(ks[11], (DEPTH, POOL_WIDTH), 0.1),
        "w_out": nrm(ks[12], (DEPTH, MIX_WIDTH, D_MODEL), MIX_WIDTH ** -0.5),
        "w_mlp_up": nrm(ks[13], (DEPTH, D_MODEL, D_FF), D_MODEL ** -0.5),
        "w_mlp_down": nrm(ks[14], (DEPTH, D_FF, D_MODEL), D_FF ** -0.5),
        "final_norm_w": 1.0 + nrm(ks[15], (D_MODEL,), 0.02),
    }


def reference(x, c, ctx, c_ctx, norm_attn_w, norm_mlp_w, w_ada, b_ada, w_in, attn_sink,
              pool_w, pool_scale, w_out, w_mlp_up, w_mlp_down, final_norm_w):
    seq_len = x.shape[1]
    row, col = axial_positions(seq_len)
    for layer in range(DEPTH):
        sh_a, sc_a, g_a, sh_m, sc_m, g_m = ada_modulation(c, w_ada[layer], b_ada[layer])
        csh_a, csc_a, cg_a, csh_m, csc_m, cg_m = ada_modulation(c_ctx, w_ada[layer], b_ada[layer])

        h = modulate(rms_norm(x, norm_attn_w[layer]), sh_a, sc_a)
        hc = modulate(rms_norm(ctx, norm_attn_w[layer]), csh_a, csc_a)
        q, k, v, u = split_projection(h @ w_in[layer])
        qc, kc, vc, uc = split_projection(hc @ w_in[layer])
        q = rope_2d(q, row, col)
        k = rope_2d(k, row, col)
        attn = latent_window_attention(q, k, v, kc, vc, attn_sink[layer])
        pool = multiscale_pool(u, pool_w[layer], pool_scale[layer])
        x = x + g_a * (jnp.concatenate([attn, pool], axis=-1) @ w_out[layer])

        hm = modulate(rms_norm(x, norm_mlp_w[layer]), sh_m, sc_m)
        x = x + g_m * squared_relu_mlp(hm, w_mlp_up[layer], w_mlp_down[layer])

        if layer < DEPTH - 1:
            attn_c = context_attention(qc, kc, vc, attn_sink[layer])
            pool_c = multiscale_pool(uc, pool_w[layer], pool_scale[layer])
            ctx = ctx + cg_a * (jnp.concatenate([attn_c, pool_c], axis=-1) @ w_out[layer])
            hcm = modulate(rms_norm(ctx, norm_mlp_w[layer]), csh_m, csc_m)
            ctx = ctx + cg_m * squared_relu_mlp(hcm, w_mlp_up[layer], w_mlp_down[layer])
    return rms_norm(x, final_norm_w)
```

```python
import numpy as np
import ml_dtypes
import concourse.bass as bass
import concourse.mybir as mybir
from concourse.bass_utils import run_bass_kernel_spmd

F32 = mybir.dt.float32
BF16 = mybir.dt.bfloat16
AF = mybir.ActivationFunctionType
ALU = mybir.AluOpType

D = 2048
L = 8192
NCORES = 8
TOK = 1024
NT = 1536
DFF = 8192
EPS = 1e-6
WINS = (2, 4, 8, 16)
KB = 1024
ARENA_WORDS = 52992


def _dmap(r):
    blk, idx = divmod(r, 32)
    b, part = blk % 2, blk // 2
    d = (idx if idx < 16 else idx + 16) + 16 * part
    return b, d, part, idx


class Ev:
    __slots__ = ("sem", "val", "eng")

    def __init__(self, sem, val, eng):
        self.sem, self.val, self.eng = sem, val, eng


class Res:
    __slots__ = ("w", "rd", "dead")

    def __init__(self, inherit=None):
        self.w = None
        self.rd = dict(inherit) if inherit else {}
        self.dead = False


class Buf:
    def __init__(self, name, ap, off, end, inherit):
        self.name, self.ap, self.off, self.end = name, ap, off, end
        self.inherit = inherit
        self.rs = {}
        self.dead = False

    def r(self, *key):
        assert not self.dead, self.name
        if key not in self.rs:
            self.rs[key] = Res(self.inherit)
        return self.rs[key]

    def events(self):
        ev = dict(self.inherit)
        for rs in self.rs.values():
            for e in ([rs.w] if rs.w is not None else []) + list(rs.rd.values()):
                k = id(e.sem)
                if k not in ev or ev[k].val < e.val:
                    ev[k] = e
        return ev


class Ctx:
    ENG = ("pe", "act", "dve", "pool", "sp")

    def __init__(self, nc, es):
        self.nc = nc
        self.e = {"pe": nc.tensor, "act": nc.scalar, "dve": nc.vector, "pool": nc.gpsimd, "sp": nc.sync}
        self.sem = {k: es.enter_context(nc.semaphore("s_" + k)) for k in self.ENG}
        self.cnt = {k: 0 for k in self.ENG}
        self.seen = {k: {} for k in self.ENG}
        self.es = es
        self.streams = {k: [] for k in self.ENG}
        self.semname = {}

    def check_deadlock(self):
        sems = {}
        pos = {k: 0 for k in self.ENG}
        prog = True
        while prog:
            prog = False
            for k in self.ENG:
                st = self.streams[k]
                while pos[k] < len(st):
                    kind, sid, val = st[pos[k]]
                    if kind == "wait":
                        if sems.get(sid, 0) < val:
                            break
                    else:
                        sems[sid] = sems.get(sid, 0) + val
                    pos[k] += 1
                    prog = True
        stuck = {k: (pos[k], len(self.streams[k]), self.streams[k][pos[k]]) for k in self.ENG if pos[k] < len(self.streams[k])}
        return stuck, sems

    def dsem(self, name):
        return {"sem": self.es.enter_context(self.nc.semaphore(name)), "cnt": 0}

    def _wait(self, eng, ev):
        key = id(ev.sem)
        if self.seen[eng].get(key, 0) >= ev.val:
            return
        self.e[eng].wait_ge(ev.sem, ev.val)
        self.streams[eng].append(("wait", key, ev.val))
        self.seen[eng][key] = ev.val

    def _deps(self, eng, r, w, x):
        strict = eng != "pe"
        for res in r:
            assert not res.dead
            if res.w is not None:
                self._wait(eng, res.w)
        for res in x:
            if res.w is not None:
                self._wait(eng, res.w)
            for ev in res.rd.values():
                if ev.eng != eng:
                    self._wait(eng, ev)
        for res in w:
            assert not res.dead
            if res.w is not None and (strict or res.w.eng != eng):
                self._wait(eng, res.w)
            for ev in res.rd.values():
                if strict or ev.eng != eng:
                    self._wait(eng, ev)

    def _record(self, ev, r, w, x):
        for res in r:
            res.rd[id(ev.sem)] = ev
        for res in x:
            res.rd[id(ev.sem)] = ev
        for res in w:
            res.w = ev
            res.rd = {}

    def op(self, eng, fn, r=(), w=(), x=(), inc=True):
        self._deps(eng, r, w, x)
        ins = fn(self.e[eng])
        if inc:
            ins.then_inc(self.sem[eng], 1)
            self.streams[eng].append(("inc", id(self.sem[eng]), 1))
            self.cnt[eng] += 1
            ev = Ev(self.sem[eng], self.cnt[eng], eng)
        else:
            ev = Ev(self.sem[eng], self.cnt[eng] + 1, eng)
        self._record(ev, r, w, x)
        return ev

    def dma(self, q, ds, out, in_, r=(), w=()):
        self._deps(q, r, w, ())
        self.e[q].dma_start(out=out, in_=in_).then_inc(ds["sem"], 16)
        self.streams[q].append(("inc", id(ds["sem"]), 16))
        ds["cnt"] += 16
        ev = Ev(ds["sem"], ds["cnt"], None)
        self._record(ev, r, w, ())
        return ev


def att_schedule():
    acts = []
    units = []
    for up in range(2):
        units.append(("piece", "in", 3 + up))
        for oc in range(4):
            for g in (2, 0, 1):
                units.append(("uproj", up, oc, g))
            units.append(("upool", up, oc, 0))
            units.append(("upool", up, oc, 1))
            units.append(("upool", up, oc, 2))
            if oc % 2 == 1:
                units.append(("poolw", up, oc))
    nu = sum(1 for u in units if u[0] != "piece")
    done = 0
    ui = 0
    ada = 12
    for it in range(32):
        acts.append(("att", it))
        target = ((it + 1) * nu + 31) // 32
        while done < target and ui < len(units):
            acts.append(units[ui])
            if units[ui][0] != "piece":
                done += 1
            ui += 1
        if it >= 2 and (it - 2) % 4 == 0 and ada < 20:
            acts.append(("piece", "ada", ada))
            ada += 1
    while ui < len(units):
        acts.append(units[ui]); ui += 1
    while ada < 20:
        acts.append(("piece", "ada", ada)); ada += 1
    return acts


def build_program():
    from contextlib import ExitStack
    nc = bass.Bass("TRN2", target_bir_lowering=False)
    dt = lambda n, s, d=F32, k="ExternalInput": nc.dram_tensor(n, s, d, kind=k).ap()
    xall = dt("xall", [NT, D])
    w_ada = dt("w_ada", [D, 6 * D])
    w_in = dt("w_in", [D, 2560])
    w_out = dt("w_out", [D, D])
    w_up = dt("w_up", [D, DFF])
    w_dn = dt("w_dn", [DFF, D])
    pool_w = dt("pool_w", [4, 256, 256])
    cvec_d = dt("cvec", [128, 32])
    bada_d = dt("bada", [128, 96])
    nw_d = dt("nw", [128, 48])
    pscale_d = dt("pscale", [128, 8])
    misc_d = dt("misc", [128, 68])
    sink_d = dt("sink16", [1, 16])
    rope_d = dt("rope", [128, 2, 1280])
    masks_d = dt("masks", [128, 4, 512], BF16)
    ident_d = dt("ident", [128, 128])
    out_d = dt("out", [TOK, D], F32, "ExternalOutput")

    with ExitStack() as es:
        K = Ctx(nc, es)
        arena = nc.alloc_sbuf_tensor("arena", [128, ARENA_WORDS], F32)
        live = []

        def mkbuf(name, off, shape, dtype):
            nb = int(np.prod(shape[1:])) * (2 if dtype == BF16 else 4)
            assert off % 4 == 0 and nb % 4 == 0 and off + nb <= ARENA_WORDS * 4, (name, off, nb)
            v = arena[0:shape[0], off // 4:(off + nb) // 4]
            if dtype != F32:
                v = v.bitcast(dtype)
            if len(shape) == 3:
                v = v.rearrange("p (a b) -> p a b", a=shape[1])
            elif len(shape) == 4:
                v = v.rearrange("p (a b c) -> p a b c", a=shape[1], b=shape[2])
            elif len(shape) == 5:
                v = v.rearrange("p (a b c d) -> p a b c d", a=shape[1], b=shape[2], c=shape[3])
            inherit = {}
            for ob in list(live):
                if ob.off < off + nb and off < ob.end:
                    if not ob.dead:
                        ob.final_events = ob.events()
                        ob.dead = True
                        for rs in ob.rs.values():
                            rs.dead = True
                    for k, e in ob.final_events.items():
                        if k not in inherit or inherit[k].val < e.val:
                            inherit[k] = e
                    if off <= ob.off and ob.end <= off + nb:
                        live.remove(ob)
            b = Buf(name, v, off, off + nb, inherit)
            fl = arena[0:shape[0], off // 4:(off + nb) // 4]
            b.flat = fl if dtype == F32 else fl.bitcast(dtype)
            live.append(b)
            return b

        pos = [0]

        def calloc(name, shape, dtype):
            nb = int(np.prod(shape[1:])) * (2 if dtype == BF16 else 4)
            nb = (nb + 31) // 32 * 32
            b = mkbuf(name, pos[0], shape, dtype)
            pos[0] += nb
            return b

        ident = calloc("ident", [128, 128], F32)
        ones_bf = calloc("ones", [128, 128], BF16)
        cvec = calloc("cvec", [128, 32], F32)
        cs = calloc("cs", [128, 16, 2], BF16)
        bada = calloc("bada", [128, 96], F32)
        nw = calloc("nw", [128, 48], F32)
        pscale = calloc("pscale", [128, 8], F32)
        misc = calloc("misc", [128, 68], F32)
        modc = calloc("modc", [128, 6, 16], F32)
        modx = calloc("modx", [128, 6, 16], F32)
        Avec = calloc("Avec", [128, 3, 16], F32)
        masks = calloc("masks", [128, 4, 512], BF16)
        sink16 = calloc("sink16", [1, 16], F32)
        esink16 = calloc("esink16", [1, 16], F32)
        selz = calloc("selz", [128, 2, 128], BF16)
        m_sb = calloc("m_sb", [2, 512], F32)
        hU = calloc("hU", [128, 16, 16], BF16)
        pw = calloc("pw", [128, 4, 2, 256], BF16)
        wring = [calloc("wring%d" % i, [128, 16, 512], BF16) for i in range(3)]
        DYN = pos[0]
        assert DYN + 144 * KB <= ARENA_WORDS * 4, DYN
        dv = lambda name, off, shape, dtype: mkbuf(name, DYN + int(off), shape, dtype)

        banks = [es.enter_context(nc.psum_tensor("pb%d" % i, [128, 512], F32)) for i in range(8)]
        bres = [Res() for _ in range(8)]
        wsem = [K.dsem("dw%d" % i) for i in range(3)]
        csem = K.dsem("dconst")
        stsem = [K.dsem("dst%d" % i) for i in range(4)]
        osem = [K.dsem("do%d" % i) for i in range(4)]

        cbufs = (ident, cvec, bada, nw, pscale, misc, sink16, masks)
        for b, src in zip(cbufs, (ident_d, cvec_d, bada_d, nw_d, pscale_d, misc_d, sink_d, masks_d)):
            K.dma("sp", csem, b.ap, src, w=[b.r()])
        cev = Ev(csem["sem"], csem["cnt"], None)
        for b in cbufs:
            b.r().w = cev
        K.dma("pool", K.dsem("dpw"), pw.ap, pool_w.rearrange("g (k p) n -> p g k n", p=128), w=[pw.r()])
        K.op("dve", lambda e: e.memset(ones_bf.ap, 1.0), w=[ones_bf.r()])
        K.op("dve", lambda e: e.memset(selz.ap, 0.0), w=[selz.r()])
        K.op("dve", lambda e: e.memset(selz.ap[0:1, 0, 64:128], 1.0), w=[selz.r()])
        K.op("dve", lambda e: e.memset(selz.ap[0:1, 1, 0:64], 1.0), w=[selz.r()])
        K.op("act", lambda e: e.activation(out=cs.ap, in_=cvec.ap.rearrange("p (k t) -> p k t", t=2), func=AF.Silu),
             r=[cvec.r()], w=[cs.r()])

        pieces = [("ada", j) for q in range(4) for j in (q, 4 + q)]
        pieces += [("in", 2), ("ada", 8), ("in", 0), ("ada", 9), ("in", 1), ("ada", 10), ("ada", 11)]
        ATT_SCHED = att_schedule()
        for act in ATT_SCHED:
            if act[0] == "piece":
                pieces.append((act[1], act[2]))
        for j in range(4):
            pieces += [("ada", 20 + j), ("out", j)]
        for p in range(4):
            pieces += [("up", p * 4 + j) for j in range(4)]
            pieces += [("dn", p * 4 + j) for j in range(4)]

        def piece_src(kind, j):
            if kind == "ada":
                m = w_ada[:, j * 512:(j + 1) * 512]
            elif kind == "in":
                m = w_in[:, j * 512:(j + 1) * 512]
            elif kind == "out":
                m = w_out[:, j * 512:(j + 1) * 512]
            elif kind == "up":
                m = w_up[:, j * 512:(j + 1) * 512]
            else:
                p, jj = divmod(j, 4)
                m = w_dn[p * 2048:(p + 1) * 2048, jj * 512:(jj + 1) * 512]
            return m.rearrange("(k p) n -> p k n", p=128)

        state = {"issued": 0, "cur": -1, "active": None, "last": None}
        free_slots = [0, 1, 2]
        slot_of = {}

        def prefetch():
            while free_slots and state["issued"] < len(pieces) and state["issued"] <= state["cur"] + 2:
                i = state["issued"]
                s = free_slots.pop(0)
                slot_of[i] = s
                K.dma("pool", wsem[s], wring[s].ap, piece_src(*pieces[i]), w=[wring[s].r()])
                state["issued"] += 1

        def next_piece(kind, j, pin=False):
            if state["active"] is not None:
                free_slots.append(state["active"])
                state["active"] = None
            state["cur"] += 1
            i = state["cur"]
            assert pieces[i] == (kind, j), (pieces[i], kind, j)
            if i not in slot_of:
                prefetch()
            s = slot_of[i]
            if not pin:
                state["active"] = s
            state["last"] = s
            prefetch()
            return wring[s].ap, wring[s].r()

        def unpin(s):
            free_slots.append(s)
            prefetch()

        prefetch()
        bank_i = [0]

        def nb():
            b = bank_i[0] % 8
            bank_i[0] += 1
            return b

        def ada_piece(j):
            ada_flush()
            W, wr = next_piece("ada", j)
            b = nb()
            for r4 in range(4):
                for jc in range(4):
                    kc = r4 * 4 + jc
                    K.op("pe", lambda e: e.matmul(banks[b][32 * jc:32 * jc + 2, :], lhsT=cs.ap[:, kc, :], rhs=W[:, kc, :],
                                                  start=(r4 == 0), stop=(r4 == 3), tile_position=(0, 32 * jc)),
                         r=[wr, cs.r()], w=[bres[b]], inc=(kc == 15))
            K.op("act", lambda e: e.activation(out=m_sb.ap, in_=banks[b][0:2, :], func=AF.Identity),
                 x=[bres[b]], w=[m_sb.r()])
            for jc in range(1, 4):
                K.op("dve", lambda e: e.tensor_tensor(out=m_sb.ap, in0=banks[b][32 * jc:32 * jc + 2, :], in1=m_sb.ap, op=ALU.add),
                     r=[m_sb.r()], x=[bres[b]], w=[m_sb.r()])
            ada_pend.append(j)

        ada_pend = []

        def ada_flush():
            while ada_pend:
                j = ada_pend.pop(0)
                s, kq = j // 4, (j % 4) * 4
                b2 = nb()
                for c in range(4):
                    K.op("pe", lambda e: e.transpose(banks[b2][:, 2 * c:2 * c + 2], m_sb.ap[0:2, c * 128:(c + 1) * 128],
                                                     ident.ap[0:2, 0:2]),
                         r=[m_sb.r(), ident.r()], w=[bres[b2]], inc=(c == 3))
                pv = banks[b2][:, 0:8].rearrange("p (c t) -> p c t", t=2)
                for t, mm in ((0, modc), (1, modx)):
                    K.op("dve", lambda e: e.tensor_tensor(out=mm.ap[:, s, kq:kq + 4], in0=pv[:, :, t],
                                                          in1=bada.ap[:, s * 16 + kq:s * 16 + kq + 4], op=ALU.add),
                         r=[bada.r()], x=[bres[b2]], w=[mm.r(s, kq)])

        def mod_r(mm, s):
            return [mm.r(s, kq) for kq in (0, 4, 8, 12)]

        def make_A(idx, mm, nwoff, kqs=(0, 4, 8, 12)):
            s = 1 if idx < 2 else 4
            for kq in kqs:
                K.op("dve", lambda e: e.scalar_tensor_tensor(out=Avec.ap[:, idx, kq:kq + 4], in0=mm.ap[:, s, kq:kq + 4], scalar=1.0,
                                                             in1=nw.ap[:, nwoff + kq:nwoff + kq + 4], op0=ALU.add, op1=ALU.mult),
                     r=[mm.r(s, kq), nw.r()], w=[Avec.r(idx, kq)])

        hTg = [dv("hT%d" % g, 16 * KB * g, [128, 16, 512], BF16) for g in range(3)]
        stage = [dv("stage0", 48 * KB, [128, 2048], F32), dv("stage1", 56 * KB, [128, 2048], F32)]
        xTg = [dv("xTg0", 64 * KB, [128, 16, 512], F32), dv("xTg1", 96 * KB, [128, 16, 512], F32)]
        sq = [dv("sq%d" % i, (128 + i) * KB, [128, 512], BF16) for i in range(4)]
        lnb = dv("lnb", 132 * KB, [128, 512], F32)
        rstd = [dv("rstd0", 134 * KB, [128, 512], F32), dv("rstd1", 136 * KB, [128, 512], F32)]
        tmpA = [dv("tmpA0", 138 * KB, [128, 512], F32), dv("tmpA1", 140 * KB, [128, 512], F32)]
        xall_v = xall.rearrange("(b p) f -> p b f", p=128)
        sq_i = [0]

        def hT_r(g, c, c0, c1):
            return [hTg[g].r(c, bb) for bb in range(c0 // 128, (c1 + 127) // 128)]

        def load_transpose(dst, dst_r, blk, t, stg):
            h = t % len(stg)
            K.dma("sp", stsem[h], stg[h].ap, xall_v[:, blk, :], w=[stg[h].r()])
            for cq in range(4):
                b = nb()
                for ci in range(4):
                    c = cq * 4 + ci
                    K.op("pe", lambda e: e.transpose(banks[b][:, ci * 128:(ci + 1) * 128],
                                                     stg[h].ap[:, c * 128:(c + 1) * 128], ident.ap),
                         r=[stg[h].r(), ident.r()], w=[bres[b]], inc=(ci == 3))
                K.op("act", lambda e: e.activation(out=dst[:, cq * 4:(cq + 1) * 4, t * 128:(t + 1) * 128],
                                                   in_=banks[b][:, :].rearrange("p (c t) -> p c t", c=4), func=AF.Identity),
                     x=[bres[b]], w=dst_r(cq, t))

        def stats_rstd(src, src_r, out_b):
            statb = nb()
            for c in range(16):
                si = sq_i[0] % 4
                sq_i[0] += 1
                K.op("act", lambda e: e.activation(out=sq[si].ap, in_=src(c), func=AF.Square), r=src_r(c), w=[sq[si].r()])
                K.op("pe", lambda e: e.matmul(banks[statb][:, :], lhsT=ones_bf.ap, rhs=sq[si].ap, start=(c == 0), stop=(c == 15)),
                     r=[sq[si].r(), ones_bf.r()], w=[bres[statb]], inc=True)
            K.op("act", lambda e: e.activation(out=lnb.ap, in_=banks[statb][:, :], func=AF.Ln, scale=1.0 / D, bias=EPS),
                 x=[bres[statb]], w=[lnb.r()])
            K.op("act", lambda e: e.activation(out=out_b.ap, in_=lnb.ap, func=AF.Exp, scale=-0.5), r=[lnb.r()], w=[out_b.r()])

        def group_front(g, xb):
            for t in range(4):
                load_transpose(xTg[xb].ap, lambda cq, t: [xTg[xb].r(cq, t)], 4 * g + t, t, stage)
            stats_rstd(lambda c: xTg[xb].ap[:, c, :], lambda c: [xTg[xb].r(c // 4, t) for t in range(4)], rstd[xb])

        def group_h(g, xb, chunks=range(16)):
            for c in chunks:
                segs = [(0, 512, 0, modc)] if g < 2 else [(0, 256, 0, modc), (256, 512, 1, modx)]
                for si, (c0, c1, ai, mm) in enumerate(segs):
                    ti = (c + si) % 2
                    K.op("dve", lambda e: e.scalar_tensor_tensor(
                        out=tmpA[ti].ap[:, c0:c1], in0=xTg[xb].ap[:, c, c0:c1], scalar=Avec.ap[:, ai, c:c + 1],
                        in1=rstd[xb].ap[:, c0:c1], op0=ALU.mult, op1=ALU.mult),
                         r=[xTg[xb].r(c // 4, t) for t in range(c0 // 128, c1 // 128)] + [rstd[xb].r(), Avec.r(ai, (c // 4) * 4)], w=[tmpA[ti].r()])
                    K.op("act", lambda e: e.activation(
                        out=hTg[g].ap[:, c, c0:c1], in_=tmpA[ti].ap[:, c0:c1], func=AF.Identity,
                        bias=mm.ap[:, 0, c:c + 1]), r=[tmpA[ti].r(), mm.r(0, (c // 4) * 4)], w=hT_r(g, c, c0, c1))

        group_front(2, 0)
        group_front(0, 1)
        for q in range(4):
            ada_piece(q)
            ada_piece(4 + q)
            ada_flush()
            make_A(0, modc, 0, kqs=(4 * q,))
            make_A(1, modx, 0, kqs=(4 * q,))
            group_h(2, 0, range(4 * q, 4 * q + 4))
            group_h(0, 1, range(4 * q, 4 * q + 4))
        group_front(1, 0)
        group_h(1, 0)

        qT = dv("qT", 48 * KB, [128, 8, TOK], BF16)
        kTz = dv("kTz", 64 * KB, [128, 4, NT], BF16)
        vsb = dv("vsb", 76 * KB, [128, 12, 2, 2, 128], BF16)
        t1 = [dv("t1_0", 105 * KB, [128, 512], F32), dv("t1_1", 107 * KB, [128, 512], F32)]
        t2 = [dv("t2_0", 109 * KB, [128, 512], F32), dv("t2_1", 111 * KB, [128, 512], F32)]
        rope = dv("rope", 113 * KB, [128, 2, 1280], F32)
        ksum = [dv("ksum0", 123 * KB, [128, 512], F32), dv("ksum1", 125 * KB, [128, 512], F32)]
        mixP = dv("mixP", 128 * KB, [128, 8, TOK], BF16)
        K.dma("sp", K.dsem("drope"), rope.ap, rope_d, w=[rope.r()])
        K.op("pool", lambda e: e.memset(vsb.flat, 1.0), w=[vsb.r()])
        vsb_all = vsb.r()
        ks_i = [0]
        rope_i = [0]

        def rope_evac(b, cols):
            i = rope_i[0] % 2
            rope_i[0] += 1
            n = cols.stop - cols.start
            K.op("dve", lambda e: e.tensor_tensor(out=t1[i].ap[:, 0:n], in0=banks[b][:, 0:n], in1=rope.ap[:, 0, cols], op=ALU.mult),
                 r=[rope.r()], x=[bres[b]], w=[t1[i].r()])
            for hf in range(2):
                po = (1 - hf) * 64
                K.op("dve", lambda e: e.tensor_tensor(out=t2[i].ap[hf * 64:(hf + 1) * 64, 0:n], in0=banks[b][po:po + 64, 0:n],
                                                      in1=rope.ap[hf * 64:(hf + 1) * 64, 1, cols], op=ALU.mult),
                     r=[rope.r()], x=[bres[b]], w=[t2[i].r(hf)])
            return i

        def proj_g(W, wr, ocs, g, c0, c1, b):
            for kc in range(16):
                K.op("pe", lambda e: e.matmul(banks[b][:, 0:c1 - c0], lhsT=W[:, kc, ocs], rhs=hTg[g].ap[:, kc, c0:c1],
                                              start=(kc == 0), stop=(kc == 15)),
                     r=[wr] + hT_r(g, kc, c0, c1), w=[bres[b]], inc=(kc == 15))

        W, wr = next_piece("in", 2)
        for oc in range(2):
            ocs = slice(oc * 128, (oc + 1) * 128)
            ab = {}
            for g in (2, 0, 1):
                ab[g] = nb()
                proj_g(W, wr, ocs, g, 0, 512, ab[g])
            for g, (c0, c1) in ((2, (1024, 1280)), (0, (0, 512)), (1, (512, 1024))):
                i = rope_evac(ab[g], slice(c0, c1))
                n = c1 - c0
                ki = ks_i[0] % 2
                ks_i[0] += 1
                K.op("pool", lambda e: e.tensor_tensor(out=ksum[ki].ap[:, 0:n], in0=t1[i].ap[:, 0:n], in1=t2[i].ap[:, 0:n], op=ALU.add),
                     r=[t1[i].r(), t2[i].r(0), t2[i].r(1)], w=[ksum[ki].r()])
                for b2 in range(2):
                    hh = 2 * oc + b2
                    K.op("pool", lambda e: e.tensor_scalar(out=kTz.ap[:, hh, c0:c1], in0=ksum[ki].ap[:, 0:n], scalar1=misc.ap[:, 66 + b2:67 + b2],
                                                           scalar2=1.0, op0=ALU.mult, op1=ALU.mult),
                         r=[ksum[ki].r(), misc.r()], w=[kTz.r(hh, g)])
            for b2 in range(2):
                hh = 2 * oc + b2
                K.op("act", lambda e: e.activation(out=kTz.ap[:, hh, 1280:1536], in_=banks[ab[2]][:, 256:512], func=AF.Identity,
                                                   scale=misc.ap[:, 66 + b2:67 + b2]),
                     r=[misc.r()], x=[bres[ab[2]]], w=[kTz.r(hh, 3)])

        def kTz_r(hh, kb):
            gi = 0 if kb < 4 else 1 if kb < 8 else 2 if kb < 10 else 3
            return [kTz.r(hh, gi)]

        for tb in [8, 9, 10, 11, 0, 1, 2, 3, 4, 5, 6, 7]:
            b = nb()
            g, tl = tb // 4, tb % 4
            for kc in range(16):
                K.op("pe", lambda e: e.matmul(banks[b][:, 0:256], lhsT=hTg[g].ap[:, kc, tl * 128:(tl + 1) * 128],
                                              rhs=W[:, kc, 256:512], start=(kc == 0), stop=(kc == 15)),
                     r=[wr, hTg[g].r(kc, tl)], w=[bres[b]], inc=(kc == 15))
            pvv = banks[b][:, 0:256].rearrange("p (a b d) -> p a b d", a=2, b=2)
            for par in range(2):
                K.op("act", lambda e: e.activation(out=vsb.ap[:, tb, :, par, par * 64:(par + 1) * 64], in_=pvv[:, :, par, :], func=AF.Identity),
                     r=[vsb_all], x=[bres[b]], w=[vsb.r(tb, par)])
        K.op("pool", lambda e: e.tensor_copy(out=hU.ap, in_=hTg[2].ap[:, :, 120:136]),
             r=[hTg[2].r(c, bb) for c in range(16) for bb in (0, 1)], w=[hU.r()])
        ada_piece(8)

        for qp in range(2):
            W, wr = next_piece("in", qp)
            for oc in range(4):
                ch = qp * 4 + oc
                ocs = slice(oc * 128, (oc + 1) * 128)
                for g in range(2):
                    b = nb()
                    proj_g(W, wr, ocs, g, 0, 512, b)
                    i = rope_evac(b, slice(g * 512, (g + 1) * 512))
                    K.op("pool", lambda e: e.tensor_tensor(out=qT.ap[:, ch, g * 512:(g + 1) * 512], in0=t1[i].ap, in1=t2[i].ap, op=ALU.add),
                         r=[t1[i].r(), t2[i].r(0), t2[i].r(1)], w=[qT.r(ch, g)])
            ada_piece(9 + qp)
        ada_piece(11)

        mixA = dv("mixA", 32 * KB, [128, 8, TOK], BF16)
        pooledT = dv("pooledT", 88 * KB, [128, 2, TOK], BF16)
        Ub = [dv("U0", 92 * KB, [128, 1040], F32), dv("U1", 123 * KB, [128, 1040], F32)]
        scr = [dv("scrA", 92 * KB + 4160, [128, 1040], F32), dv("scrB", 92 * KB + 8320, [128, 1040], F32)]
        PT = [dv("PT%d" % i, (105 + i) * KB, [128, 512], BF16) for i in range(10)]
        rec = [dv("rec0", 115 * KB, [128, 512], F32), dv("rec1", 117 * KB, [128, 512], F32)]
        esinkz = dv("esinkz", 119 * KB, [128, 16, 128], BF16)
        K.op("act", lambda e: e.activation(out=esink16.ap, in_=sink16.ap, func=AF.Exp), r=[sink16.r()], w=[esink16.r()])
        K.op("dve", lambda e: e.memset(esinkz.flat, 0.0), w=[esinkz.r()])
        K.op("dve", lambda e: e.tensor_copy(out=esinkz.ap[0:1, :, :], in_=esink16.ap.unsqueeze(2).broadcast_to([1, 16, 128])),
             r=[esink16.r()], w=[esinkz.r()])
        stb, otb, mib = [0, 1, 2, 3, 4], [5, 6], [7]
        st_i, pt_i, ot_i, mi_i = [0], [0], [0], [0]

        def nbm():
            b = mib[mi_i[0] % 1]
            mi_i[0] += 1
            return b

        def key_blocks(n):
            return [(n - 1 if n > 0 else 8, 0 if n == 0 else 1), (n, None), (n + 1 if n < 7 else 9, 3 if n == 7 else 2),
                    (10, None), (11, None)]

        def emit_st(n, h):
            c0 = 4 * (h // 2)
            tiles = []
            for (kb, mk) in key_blocks(n):
                b = stb[st_i[0] % 5]
                st_i[0] += 1
                K.op("pe", lambda e: e.matmul(banks[b][:, :], lhsT=kTz.ap[:, h, kb * 128:(kb + 1) * 128],
                                              rhs=qT.ap[:, c0:c0 + 4, n * 128:(n + 1) * 128], start=True, stop=True),
                     r=kTz_r(h, kb) + [qT.r(c0 + g, n // 4) for g in range(4)], w=[bres[b]])
                p = pt_i[0] % 10
                pt_i[0] += 1
                K.op("act", lambda e: e.activation(out=PT[p].ap, in_=banks[b][:, :], func=AF.Exp, scale=0.125),
                     x=[bres[b]], w=[PT[p].r()])
                if mk is not None:
                    K.op("dve", lambda e: e.tensor_tensor(out=PT[p].ap, in0=PT[p].ap, in1=masks.ap[:, mk, :], op=ALU.mult),
                         r=[masks.r()], w=[PT[p].r()])
                tiles.append((kb, p))
            return tiles

        def emit_pv(n, h, tiles):
            hb = (h % 2) * 64
            c0 = 4 * (h // 2)
            b = otb[ot_i[0] % 2]
            ri = ot_i[0] % 2
            ot_i[0] += 1
            K.op("pe", lambda e: e.matmul(banks[b][:, :], lhsT=selz.ap[:, h % 2, :], rhs=esinkz.ap[:, h * 4:(h + 1) * 4, :], start=True, stop=False),
                 r=[selz.r(), esinkz.r()], w=[bres[b]], inc=False)
            for ti, (kb, p) in enumerate(tiles):
                K.op("pe", lambda e: e.matmul(banks[b][:, :], lhsT=vsb.ap[:, kb, h // 2, h % 2, :], rhs=PT[p].ap, start=False, stop=(ti == 4)),
                     r=[vsb.r(kb, 0), vsb.r(kb, 1), PT[p].r()], w=[bres[b]], inc=(ti == 4))
            ob, db = hb, 64 - hb
            K.op("act", lambda e: e.activation(out=rec[ri].ap[db:db + 64, :], in_=banks[b][db:db + 64, :], func=AF.Ln),
                 x=[bres[b]], w=[rec[ri].r()])
            K.op("act", lambda e: e.activation(out=rec[ri].ap[db:db + 64, :], in_=rec[ri].ap[db:db + 64, :], func=AF.Exp, scale=-1.0),
                 r=[rec[ri].r()], w=[rec[ri].r()])
            K.op("dve", lambda e: e.tensor_tensor(out=mixA.ap[ob:ob + 64, c0:c0 + 4, n * 128:(n + 1) * 128],
                                                  in0=banks[b][ob:ob + 64, :].rearrange("p (g q) -> p g q", g=4),
                                                  in1=rec[ri].ap[db:db + 64, :].rearrange("p (g q) -> p g q", g=4), op=ALU.mult),
                 r=[rec[ri].r()], x=[bres[b]], w=[mixA.r(h, n)])

        hflag = misc.ap[:, 0:2]
        invc = misc.ap[:, 2:66].rearrange("p (g t) -> p g t", g=4)
        ust = {}

        def u_proj(up, oc, g):
            W, wr = ust["W"]
            uc = up * 4 + oc
            U = Ub[uc % 2]
            ocs = slice(oc * 128, (oc + 1) * 128)
            b = nbm()
            if g < 2:
                proj_g(W, wr, ocs, g, 0, 512, b)
                K.op("act", lambda e: e.activation(out=U.ap[:, 8 + g * 512:8 + (g + 1) * 512], in_=banks[b][:, :], func=AF.Identity),
                     x=[bres[b]], w=[U.r(g)])
            else:
                for kc in range(16):
                    K.op("pe", lambda e: e.matmul(banks[b][:, 0:16], lhsT=W[:, kc, ocs], rhs=hU.ap[:, kc, :], start=(kc == 0), stop=(kc == 15)),
                         r=[wr, hU.r()], w=[bres[b]], inc=(kc == 15))
                K.op("act", lambda e: e.activation(out=U.ap[:, 0:8], in_=banks[b][:, 0:8], func=AF.Identity, scale=hflag[:, 0:1]),
                     r=[misc.r()], x=[bres[b]], w=[U.r(2)])
                K.op("act", lambda e: e.activation(out=U.ap[:, 1032:1040], in_=banks[b][:, 8:16], func=AF.Identity, scale=hflag[:, 1:2]),
                     r=[misc.r()], x=[bres[b]], w=[U.r(3)])

        pst = {}

        def u_pool(up, oc, part):
            uc = up * 4 + oc
            gi = uc // 2
            wsz = WINS[gi]
            U = Ub[uc % 2]
            Ur = [U.r(i) for i in range(4)]
            if part == 0:
                K.op("dve", lambda e: e.tensor_tensor(out=scr[0].ap[:, 0:1039], in0=U.ap[:, 0:1039], in1=U.ap[:, 1:1040], op=ALU.add),
                     r=Ur, w=[scr[0].r()])
                pst.update(ci=0, ln=1039, step=2)
                nsteps = 1
            elif part == 1:
                nsteps = 2
            else:
                nsteps = 0
            if part < 2:
                while pst["step"] < wsz and nsteps > 0:
                    ci, ln, step = pst["ci"], pst["ln"], pst["step"]
                    nl = ln - step
                    K.op("dve", lambda e: e.tensor_tensor(out=scr[1 - ci].ap[:, 0:nl], in0=scr[ci].ap[:, 0:nl],
                                                          in1=scr[ci].ap[:, step:step + nl], op=ALU.add),
                         r=[scr[ci].r()], w=[scr[1 - ci].r()])
                    pst.update(ci=1 - ci, ln=nl, step=step * 2)
                    nsteps -= 1
                return
            ci = pst["ci"]
            assert pst["step"] >= wsz
            off = 8 - wsz // 2
            ps = oc % 2
            K.op("dve", lambda e: e.scalar_tensor_tensor(out=pooledT.ap[:, ps, :], in0=scr[ci].ap[:, off:off + 1024], scalar=1.0 / wsz,
                                                         in1=U.ap[:, 8:1032], op0=ALU.mult, op1=ALU.subtract),
                 r=[scr[ci].r()] + Ur, w=[pooledT.r(ps)])
            for (ta, ia) in ((0, 0), (1016, 8)):
                K.op("dve", lambda e: e.tensor_tensor(out=scr[1 - ci].ap[:, 0:8], in0=scr[ci].ap[:, off + ta:off + ta + 8],
                                                      in1=invc[:, gi, ia:ia + 8], op=ALU.mult),
                     r=[scr[ci].r(), misc.r()], w=[scr[1 - ci].r()])
                K.op("dve", lambda e: e.tensor_tensor(out=pooledT.ap[:, ps, ta:ta + 8], in0=scr[1 - ci].ap[:, 0:8],
                                                      in1=U.ap[:, 8 + ta:16 + ta], op=ALU.subtract),
                     r=[scr[1 - ci].r()] + Ur, w=[pooledT.r(ps)])

        def u_poolw(up, oc):
            gi = (up * 4 + oc) // 2
            for po in range(2):
                for g2 in range(2):
                    b = nbm()
                    for k2 in range(2):
                        K.op("pe", lambda e: e.matmul(banks[b][:, :], lhsT=pw.ap[:, gi, k2, po * 128:(po + 1) * 128],
                                                      rhs=pooledT.ap[:, k2, g2 * 512:(g2 + 1) * 512], start=(k2 == 0), stop=(k2 == 1)),
                             r=[pw.r(), pooledT.r(k2)], w=[bres[b]], inc=(k2 == 1))
                    K.op("act", lambda e: e.activation(out=mixP.ap[:, 2 * gi + po, g2 * 512:(g2 + 1) * 512], in_=banks[b][:, :],
                                                       func=AF.Identity, scale=pscale.ap[:, 2 * gi + po:2 * gi + po + 1]),
                         r=[pscale.r()], x=[bres[b]], w=[mixP.r(2 * gi + po, g2)])

        nb_save = nb
        nb = nbm
        prev = None
        for act in ATT_SCHED:
            if act[0] == "att":
                n, h = divmod(act[1], 4)
                tiles = emit_st(n, h)
                if prev is not None:
                    emit_pv(*prev)
                prev = (n, h, tiles)
            elif act[0] == "piece":
                if act[1] == "ada":
                    ada_piece(act[2])
                else:
                    if "slot" in ust:
                        unpin(ust["slot"])
                    ust["W"] = next_piece("in", act[2], pin=True)
                    ust["slot"] = state["last"]
            elif act[0] == "uproj":
                u_proj(*act[1:])
            elif act[0] == "upool":
                u_pool(*act[1:])
            elif act[0] == "poolw":
                u_poolw(*act[1:])
        emit_pv(*prev)
        unpin(ust["slot"])
        nb = nb_save

        stage2 = [dv("stage%db" % i, 8 * i * KB, [128, 2048], F32) for i in range(4)]
        x1T = dv("x1T", 64 * KB, [128, 16, TOK], F32)
        sq = [dv("sq%db" % i, (48 + i) * KB, [128, 512], BF16) for i in range(4)]
        lnb = dv("lnbb", 52 * KB, [128, 512], F32)
        rstd = [dv("rstd0b", 54 * KB, [128, 512], F32), dv("rstd1b", 56 * KB, [128, 512], F32)]
        tmpA = [dv("tmpA0b", 58 * KB, [128, 512], F32), dv("tmpA1b", 60 * KB, [128, 512], F32)]
        for t in range(8):
            load_transpose(x1T.ap, lambda cq, t: [x1T.r(c, t) for c in range(cq * 4, cq * 4 + 4)], t, t, stage2)

        def mix_rhs(kc, g2):
            if kc < 8:
                return (mixA.ap[:, kc, g2 * 512:(g2 + 1) * 512],
                        [mixA.r(2 * (kc // 4) + hh, n) for hh in range(2) for n in range(4 * g2, 4 * g2 + 4)])
            return mixP.ap[:, kc - 8, g2 * 512:(g2 + 1) * 512], [mixP.r(kc - 8, g2)]

        def x1_r(c, g2):
            return [x1T.r(c, t) for t in range(4 * g2, 4 * g2 + 4)]

        ring6 = [0]
        pend = []

        def nb6():
            b = ring6[0] % 6
            ring6[0] += 1
            return b

        def flush_stats():
            while pend:
                OC, g2, si = pend.pop(0)
                K.op("pe", lambda e: e.matmul(banks[6 + g2][:, :], lhsT=ones_bf.ap, rhs=sq[si].ap, start=(OC == 0), stop=(OC == 15)),
                     r=[sq[si].r(), ones_bf.r()], w=[bres[6 + g2]], inc=True)
                if OC == 15:
                    K.op("act", lambda e: e.activation(out=lnb.ap, in_=banks[6 + g2][:, :], func=AF.Ln, scale=1.0 / D, bias=EPS),
                         x=[bres[6 + g2]], w=[lnb.r()])
                    K.op("act", lambda e: e.activation(out=rstd[g2].ap, in_=lnb.ap, func=AF.Exp, scale=-0.5), r=[lnb.r()], w=[rstd[g2].r()])

        def proj2(W, wr, oc, rhs_fn, OC, gate_s, stats=False, scale_fw=False):
            ab = [nb6(), nb6()] if stats else [nb(), nb()]
            for g2 in range(2):
                for kc in range(16):
                    b = ab[g2]
                    rhs, rr = rhs_fn(kc, g2)
                    K.op("pe", lambda e: e.matmul(banks[b][:, :], lhsT=W[:, kc, oc * 128:(oc + 1) * 128], rhs=rhs,
                                                  start=(kc == 0), stop=(kc == 15)),
                         r=[wr] + rr, w=[bres[b]], inc=(kc == 15))
            flush_stats()
            for g2 in range(2):
                b = ab[g2]
                K.op("dve", lambda e: e.scalar_tensor_tensor(
                    out=x1T.ap[:, OC, g2 * 512:(g2 + 1) * 512], in0=banks[b][:, :], scalar=modc.ap[:, gate_s, OC:OC + 1],
                    in1=x1T.ap[:, OC, g2 * 512:(g2 + 1) * 512], op0=ALU.mult, op1=ALU.add),
                     r=x1_r(OC, g2) + [modc.r(gate_s, (OC // 4) * 4)], x=[bres[b]], w=x1_r(OC, g2))
                if stats:
                    si = sq_i[0] % 4
                    sq_i[0] += 1
                    K.op("act", lambda e: e.activation(out=sq[si].ap, in_=x1T.ap[:, OC, g2 * 512:(g2 + 1) * 512], func=AF.Square),
                         r=x1_r(OC, g2), w=[sq[si].r()])
                    pend.append((OC, g2, si))
                    if scale_fw:
                        K.op("dve", lambda e: e.tensor_scalar(out=x1T.ap[:, OC, g2 * 512:(g2 + 1) * 512], in0=x1T.ap[:, OC, g2 * 512:(g2 + 1) * 512],
                                                              scalar1=nw.ap[:, 32 + OC:33 + OC], scalar2=None, op0=ALU.mult),
                             r=[nw.r()], w=x1_r(OC, g2))

        nb_save = nb
        nb = nb6
        for j in range(4):
            ada_piece(20 + j)
            W, wr = next_piece("out", j)
            for oc in range(4):
                proj2(W, wr, oc, mix_rhs, j * 4 + oc, 2, stats=True)
        flush_stats()
        ada_flush()
        nb = nb_save
        make_A(2, modc, 16)

        hmT = dv("hmT", 16 * KB, [128, 16, TOK], BF16)
        for g2 in range(2):
            for c in range(16):
                ti = c % 2
                K.op("dve", lambda e: e.scalar_tensor_tensor(
                    out=tmpA[ti].ap, in0=x1T.ap[:, c, g2 * 512:(g2 + 1) * 512], scalar=Avec.ap[:, 2, c:c + 1], in1=rstd[g2].ap,
                    op0=ALU.mult, op1=ALU.mult), r=x1_r(c, g2) + [rstd[g2].r(), Avec.r(2, (c // 4) * 4)], w=[tmpA[ti].r()])
                K.op("act", lambda e: e.activation(out=hmT.ap[:, c, g2 * 512:(g2 + 1) * 512], in_=tmpA[ti].ap, func=AF.Identity,
                                                   bias=modc.ap[:, 3, c:c + 1]), r=[tmpA[ti].r()] + mod_r(modc, 3), w=[hmT.r(c, g2)])
        aTh = [dv("aT_lo", 48 * KB, [128, 8, TOK], BF16), dv("aT_hi", 128 * KB, [128, 8, TOK], BF16)]
        rl = [dv("rl0", 0, [128, 512], F32), dv("rl1", 2 * KB, [128, 512], F32)]
        sq = [dv("sq%dc" % i, (4 + i) * KB, [128, 512], BF16) for i in range(4)]
        lnb = dv("lnbc", 8 * KB, [128, 512], F32)
        rstd = [dv("rstd0c", 10 * KB, [128, 512], F32), dv("rstd1c", 12 * KB, [128, 512], F32)]
        rl_i = [0]

        def aT_rhs(kc, g2):
            return aTh[kc // 8].ap[:, kc % 8, g2 * 512:(g2 + 1) * 512], [aTh[kc // 8].r(kc % 8, g2)]

        for p in range(4):
            for j in range(4):
                W, wr = next_piece("up", p * 4 + j)
                for oc in range(4):
                    fc = j * 4 + oc
                    ab = [nb6(), nb6()] if p == 3 else [nb(), nb()]
                    for g2 in range(2):
                        for kc in range(16):
                            b = ab[g2]
                            K.op("pe", lambda e: e.matmul(banks[b][:, :], lhsT=W[:, kc, oc * 128:(oc + 1) * 128],
                                                          rhs=hmT.ap[:, kc, g2 * 512:(g2 + 1) * 512], start=(kc == 0), stop=(kc == 15)),
                                 r=[wr, hmT.r(kc, g2)], w=[bres[b]], inc=(kc == 15))
                    for g2 in range(2):
                        b = ab[g2]
                        ri = rl_i[0] % 2
                        rl_i[0] += 1
                        K.op("act", lambda e: e.activation(out=rl[ri].ap, in_=banks[b][:, :], func=AF.Relu), x=[bres[b]], w=[rl[ri].r()])
                        K.op("dve", lambda e: e.tensor_tensor(out=aTh[fc // 8].ap[:, fc % 8, g2 * 512:(g2 + 1) * 512], in0=rl[ri].ap, in1=rl[ri].ap, op=ALU.mult),
                             r=[rl[ri].r()], w=[aTh[fc // 8].r(fc % 8, g2)])
            for j in range(4):
                W, wr = next_piece("dn", p * 4 + j)
                for oc in range(4):
                    proj2(W, wr, oc, aT_rhs, j * 4 + oc, 5, stats=(p == 3), scale_fw=(p == 3))
        flush_stats()

        ostage = [dv("ost%d" % i, (32 + 8 * i) * KB, [128, 2048], F32) for i in range(4)]
        rcol = dv("rcol", 16 * KB, [128, 8], F32)
        bc = nb()
        for t in range(8):
            g2, tt = divmod(t, 4)
            K.op("pe", lambda e: e.transpose(banks[bc][:, t:t + 1], rstd[g2].ap[0:1, tt * 128:(tt + 1) * 128], ident.ap[0:1, 0:1]),
                 r=[rstd[g2].r(), ident.r()], w=[bres[bc]], inc=(t == 7))
        K.op("dve", lambda e: e.tensor_copy(out=rcol.ap, in_=banks[bc][:, 0:8]), x=[bres[bc]], w=[rcol.r()])
        for t in range(8):
            yi = t % 4
            for cq in range(4):
                b = nb()
                for ci in range(4):
                    c = cq * 4 + ci
                    K.op("pe", lambda e: e.transpose(banks[b][:, ci * 128:(ci + 1) * 128], x1T.ap[:, c, t * 128:(t + 1) * 128], ident.ap),
                         r=[x1T.r(c, t), ident.r()], w=[bres[b]], inc=(ci == 3))
                eng = "act" if cq % 2 == 0 else "dve"
                if eng == "act":
                    K.op("act", lambda e: e.activation(out=ostage[yi].ap[:, cq * 512:(cq + 1) * 512], in_=banks[b][:, :], func=AF.Identity,
                                                       scale=rcol.ap[:, t:t + 1]), r=[rcol.r()], x=[bres[b]], w=[ostage[yi].r(cq)])
                else:
                    K.op("dve", lambda e: e.tensor_scalar(out=ostage[yi].ap[:, cq * 512:(cq + 1) * 512], in0=banks[b][:, :],
                                                          scalar1=rcol.ap[:, t:t + 1], scalar2=None, op0=ALU.mult),
                         r=[rcol.r()], x=[bres[b]], w=[ostage[yi].r(cq)])
            K.dma("sp", osem[yi], out_d[t * 128:(t + 1) * 128, :], ostage[yi].ap, r=[ostage[yi].r(cq) for cq in range(4)])
        for d in osem:
            nc.sync.wait_ge(d["sem"], d["cnt"])
        assert state["cur"] == len(pieces) - 1
        stuck, sems = K.check_deadlock()
        if stuck:
            names = {id(K.sem[k]): "s_" + k for k in K.ENG}
            raise RuntimeError("deadlock: " + str({k: (v[0], v[1], v[2][0], names.get(v[2][1], "dma"), v[2][2], sems.get(v[2][1], 0)) for k, v in stuck.items()}))
    return nc


def _prep(inputs):
    f = lambda a: np.ascontiguousarray(np.asarray(a, dtype=np.float32))
    x = f(inputs["x"])[0]
    ctx = f(inputs["ctx"])[0]
    c = f(inputs["c"])[0]
    c_ctx = f(inputs["c_ctx"])
    w_ada = f(inputs["w_ada"])[0]
    b_ada = f(inputs["b_ada"])[0]
    w_in = f(inputs["w_in"])[0]
    w_out = f(inputs["w_out"])[0]
    rmap = [_dmap(r) for r in range(128)]
    qcols = []
    for cch in range(8):
        for r in range(128):
            b, d, _, _ = rmap[r]
            hq = 4 * (2 * (cch // 4) + b) + cch % 4
            qcols.append(hq * 64 + d)
    kcols = []
    for kc in range(2):
        for r in range(128):
            b, d, _, _ = rmap[r]
            kcols.append(1024 + (2 * kc + b) * 64 + d)
    cols = np.concatenate([np.array(qcols), np.array(kcols), np.arange(1280, 2560)])
    w_in_p = np.ascontiguousarray(w_in[:, cols])
    orow = []
    for cch in range(8):
        for b in range(2):
            hq = 4 * (2 * (cch // 4) + b) + cch % 4
            orow.append(hq * 64 + np.arange(64))
    rows = np.concatenate(orow + [np.arange(1024, 2048)])
    w_out_p = np.ascontiguousarray(w_out[rows, :])
    pm = lambda v: np.ascontiguousarray(v.reshape(-1, 128).T)
    cvec = np.stack([pm(c), pm(c_ctx)], axis=2).reshape(128, 32)
    bada = pm(b_ada)
    nw = np.concatenate([pm(f(inputs["norm_attn_w"])[0]), pm(f(inputs["norm_mlp_w"])[0]), pm(f(inputs["final_norm_w"]))], axis=1)
    pscale = pm(f(inputs["pool_scale"])[0])
    sink16 = np.ascontiguousarray(f(inputs["attn_sink"])[0][None, :])
    ident = np.eye(128, dtype=np.float32)
    jj = np.arange(128)[:, None]
    ii = np.arange(128)[None, :]
    m_prev = np.tile((ii <= jj).astype(np.float32), (1, 4))
    m_next = np.tile((jj <= ii).astype(np.float32), (1, 4))
    zeros = np.zeros_like(m_prev)
    inv_freq = (10000.0 ** (-np.arange(0, 32, 2, dtype=np.float32) / 32)).astype(np.float32)
    jfreq = np.array([rmap[r][3] % 16 for r in range(128)])
    use_col = np.array([rmap[r][3] >= 16 for r in range(128)])
    sgn = np.array([-1.0 if rmap[r][2] == 0 else 1.0 for r in range(128)], dtype=np.float32)
    shared = dict(w_ada=w_ada, w_in=w_in_p, w_out=w_out_p, w_up=f(inputs["w_mlp_up"])[0], w_dn=f(inputs["w_mlp_down"])[0],
                  pool_w=f(inputs["pool_w"])[0], cvec=cvec, bada=bada, nw=nw, pscale=pscale, sink16=sink16, ident=ident)
    in_maps = []
    zblk = np.zeros((128, D), np.float32)
    for i in range(NCORES):
        t0 = i * TOK
        prev = x[t0 - 128:t0] if i > 0 else zblk
        nxt = x[t0 + TOK:t0 + TOK + 128] if i < NCORES - 1 else zblk
        xall = np.concatenate([x[t0:t0 + TOK], prev, nxt, ctx], axis=0)
        tpos = np.concatenate([np.arange(t0, t0 + TOK), np.arange(t0 - 128, t0), np.arange(t0 + TOK, t0 + TOK + 128)])
        tpos = np.clip(tpos, 0, L - 1)
        row = (tpos // 64).astype(np.float32)
        col = (tpos % 64).astype(np.float32)
        posm = np.where(use_col[:, None], col[None, :], row[None, :]).astype(np.float32)
        ang = (posm * inv_freq[jfreq][:, None]).astype(np.float32)
        rope = np.stack([np.cos(ang), sgn[:, None] * np.sin(ang)], axis=1).astype(np.float32)
        masks = np.stack([zeros if i == 0 else m_prev, m_prev, m_next, zeros if i == NCORES - 1 else m_next], axis=1)
        misc = np.zeros((128, 68), np.float32)
        misc[:, 66] = [1.0 if rmap[r][0] == 0 else 0.0 for r in range(128)]
        misc[:, 67] = [1.0 if rmap[r][0] == 1 else 0.0 for r in range(128)]
        misc[:, 0] = 0.0 if i == 0 else 1.0
        misc[:, 1] = 0.0 if i == NCORES - 1 else 1.0
        for gi, w in enumerate(WINS):
            for ti, tl in enumerate(list(range(8)) + list(range(1016, 1024))):
                tg = t0 + tl
                lo = min(max(tg - w // 2, 0), L)
                hi = min(max(tg - w // 2 + w, 0), L)
                misc[:, 2 + gi * 16 + ti] = 1.0 / float(hi - lo)
        m = dict(shared)
        m.update(xall=np.ascontiguousarray(xall), rope=rope, masks=masks.astype(ml_dtypes.bfloat16), misc=misc)
        in_maps.append(m)
    return in_maps


def kernel(**inputs):
    in_maps = _prep(inputs)
    nc = build_program()
    resu = run_bass_kernel_spmd(nc, in_maps, core_ids=list(range(NCORES)))
    outs = [np.asarray(r["out"], dtype=np.float32) for r in resu.results]
    return np.concatenate(outs, axis=0)[None, :, :]
```

```python
import numpy as np
import ml_dtypes
import concourse.bass as bass
import concourse.mybir as mybir
from concourse.bass_utils import run_bass_kernel_spmd

F32 = mybir.dt.float32
BF16 = mybir.dt.bfloat16
AF = mybir.ActivationFunctionType
ALU = mybir.AluOpType

D = 2048
L = 8192
NCORES = 8
TOK = 1024
NT = 1536
DFF = 8192
EPS = 1e-6
WINS = (2, 4, 8, 16)
KB = 1024
ARENA_WORDS = 52992


def _dmap(r):
    blk, idx = divmod(r, 32)
    b, part = blk % 2, blk // 2
    d = (idx if idx < 16 else idx + 16) + 16 * part
    return b, d, part, idx


class Ev:
    __slots__ = ("sem", "val", "eng")

    def __init__(self, sem, val, eng):
        self.sem, self.val, self.eng = sem, val, eng


class Res:
    __slots__ = ("w", "rd", "dead")

    def __init__(self, inherit=None):
        self.w = None
        self.rd = dict(inherit) if inherit else {}
        self.dead = False


class Buf:
    def __init__(self, name, ap, off, end, inherit):
        self.name, self.ap, self.off, self.end = name, ap, off, end
        self.inherit = inherit
        self.rs = {}
        self.dead = False

    def r(self, *key):
        assert not self.dead, self.name
        if key not in self.rs:
            self.rs[key] = Res(self.inherit)
        return self.rs[key]

    def events(self):
        ev = dict(self.inherit)
        for rs in self.rs.values():
            for e in ([rs.w] if rs.w is not None else []) + list(rs.rd.values()):
                k = id(e.sem)
                if k not in ev or ev[k].val < e.val:
                    ev[k] = e
        return ev


class Ctx:
    ENG = ("pe", "act", "dve", "pool", "sp")

    def __init__(self, nc, es):
        self.nc = nc
        self.e = {"pe": nc.tensor, "act": nc.scalar, "dve": nc.vector, "pool": nc.gpsimd, "sp": nc.sync}
        self.sem = {k: es.enter_context(nc.semaphore("s_" + k)) for k in self.ENG}
        self.cnt = {k: 0 for k in self.ENG}
        self.seen = {k: {} for k in self.ENG}
        self.es = es
        self.streams = {k: [] for k in self.ENG}
        self.semname = {}

    def check_deadlock(self):
        sems = {}
        pos = {k: 0 for k in self.ENG}
        prog = True
        while prog:
            prog = False
            for k in self.ENG:
                st = self.streams[k]
                while pos[k] < len(st):
                    kind, sid, val = st[pos[k]]
                    if kind == "wait":
                        if sems.get(sid, 0) < val:
                            break
                    else:
                        sems[sid] = sems.get(sid, 0) + val
                    pos[k] += 1
                    prog = True
        stuck = {k: (pos[k], len(self.streams[k]), self.streams[k][pos[k]]) for k in self.ENG if pos[k] < len(self.streams[k])}
        return stuck, sems

    def dsem(self, name):
        return {"sem": self.es.enter_context(self.nc.semaphore(name)), "cnt": 0}

    def _wait(self, eng, ev):
        key = id(ev.sem)
        if self.seen[eng].get(key, 0) >= ev.val:
            return
        self.e[eng].wait_ge(ev.sem, ev.val)
        self.streams[eng].append(("wait", key, ev.val))
        self.seen[eng][key] = ev.val

    def _deps(self, eng, r, w, x):
        strict = eng != "pe"
        for res in r:
            assert not res.dead
            if res.w is not None:
                self._wait(eng, res.w)
        for res in x:
            if res.w is not None:
                self._wait(eng, res.w)
            for ev in res.rd.values():
                if ev.eng != eng:
                    self._wait(eng, ev)
        for res in w:
            assert not res.dead
            if res.w is not None and (strict or res.w.eng != eng):
                self._wait(eng, res.w)
            for ev in res.rd.values():
                if strict or ev.eng != eng:
                    self._wait(eng, ev)

    def _record(self, ev, r, w, x):
        for res in r:
            res.rd[id(ev.sem)] = ev
        for res in x:
            res.rd[id(ev.sem)] = ev
        for res in w:
            res.w = ev
            res.rd = {}

    def op(self, eng, fn, r=(), w=(), x=(), inc=True):
        self._deps(eng, r, w, x)
        ins = fn(self.e[eng])
        if inc:
            ins.then_inc(self.sem[eng], 1)
            self.streams[eng].append(("inc", id(self.sem[eng]), 1))
            self.cnt[eng] += 1
            ev = Ev(self.sem[eng], self.cnt[eng], eng)
        else:
            ev = Ev(self.sem[eng], self.cnt[eng] + 1, eng)
        self._record(ev, r, w, x)
        return ev

    def dma(self, q, ds, out, in_, r=(), w=()):
        self._deps(q, r, w, ())
        self.e[q].dma_start(out=out, in_=in_).then_inc(ds["sem"], 16)
        self.streams[q].append(("inc", id(ds["sem"]), 16))
        ds["cnt"] += 16
        ev = Ev(ds["sem"], ds["cnt"], None)
        self._record(ev, r, w, ())
        return ev


def att_schedule():
    acts = []
    units = []
    for up in range(2):
        units.append(("piece", "in", 3 + up))
        for oc in range(4):
            for g in (2, 0, 1):
                units.append(("uproj", up, oc, g))
            units.append(("upool", up, oc, 0))
            units.append(("upool", up, oc, 1))
            units.append(("upool", up, oc, 2))
            if oc % 2 == 1:
                units.append(("poolw", up, oc))
    nu = sum(1 for u in units if u[0] != "piece")
    done = 0
    ui = 0
    ada = 12
    for it in range(32):
        acts.append(("att", it))
        target = ((it + 1) * nu + 31) // 32
        while done < target and ui < len(units):
            acts.append(units[ui])
            if units[ui][0] != "piece":
                done += 1
            ui += 1
        if it >= 2 and (it - 2) % 4 == 0 and ada < 20:
            acts.append(("piece", "ada", ada))
            ada += 1
    while ui < len(units):
        acts.append(units[ui]); ui += 1
    while ada < 20:
        acts.append(("piece", "ada", ada)); ada += 1
    return acts


def build_program():
    from contextlib import ExitStack
    nc = bass.Bass("TRN2", target_bir_lowering=False)
    dt = lambda n, s, d=F32, k="ExternalInput": nc.dram_tensor(n, s, d, kind=k).ap()
    xall = dt("xall", [NT, D])
    w_ada = dt("w_ada", [D, 6 * D])
    w_in = dt("w_in", [D, 2560])
    w_out = dt("w_out", [D, D])
    w_up = dt("w_up", [D, DFF])
    w_dn = dt("w_dn", [DFF, D])
    pool_w = dt("pool_w", [4, 256, 256])
    cvec_d = dt("cvec", [128, 32])
    bada_d = dt("bada", [128, 96])
    nw_d = dt("nw", [128, 48])
    pscale_d = dt("pscale", [128, 8])
    misc_d = dt("misc", [128, 68])
    sink_d = dt("sink16", [1, 16])
    rope_d = dt("rope", [128, 2, 1280])
    masks_d = dt("masks", [128, 4, 512], BF16)
    ident_d = dt("ident", [128, 128])
    out_d = dt("out", [TOK, D], F32, "ExternalOutput")

    with ExitStack() as es:
        K = Ctx(nc, es)
        arena = nc.alloc_sbuf_tensor("arena", [128, ARENA_WORDS], F32)
        live = []

        def mkbuf(name, off, shape, dtype):
            nb = int(np.prod(shape[1:])) * (2 if dtype == BF16 else 4)
            assert off % 4 == 0 and nb % 4 == 0 and off + nb <= ARENA_WORDS * 4, (name, off, nb)
            v = arena[0:shape[0], off // 4:(off + nb) // 4]
            if dtype != F32:
                v = v.bitcast(dtype)
            if len(shape) == 3:
                v = v.rearrange("p (a b) -> p a b", a=shape[1])
            elif len(shape) == 4:
                v = v.rearrange("p (a b c) -> p a b c", a=shape[1], b=shape[2])
            elif len(shape) == 5:
                v = v.rearrange("p (a b c d) -> p a b c d", a=shape[1], b=shape[2], c=shape[3])
            inherit = {}
            for ob in list(live):
                if ob.off < off + nb and off < ob.end:
                    if not ob.dead:
                        ob.final_events = ob.events()
                        ob.dead = True
                        for rs in ob.rs.values():
                            rs.dead = True
                    for k, e in ob.final_events.items():
                        if k not in inherit or inherit[k].val < e.val:
                            inherit[k] = e
                    if off <= ob.off and ob.end <= off + nb:
                        live.remove(ob)
            b = Buf(name, v, off, off + nb, inherit)
            fl = arena[0:shape[0], off // 4:(off + nb) // 4]
            b.flat = fl if dtype == F32 else fl.bitcast(dtype)
            live.append(b)
            return b

        pos = [0]

        def calloc(name, shape, dtype):
            nb = int(np.prod(shape[1:])) * (2 if dtype == BF16 else 4)
            nb = (nb + 31) // 32 * 32
            b = mkbuf(name, pos[0], shape, dtype)
            pos[0] += nb
            return b

        ident = calloc("ident", [128, 128], F32)
        ones_bf = calloc("ones", [128, 128], BF16)
        cvec = calloc("cvec", [128, 32], F32)
        cs = calloc("cs", [128, 16, 2], BF16)
        bada = calloc("bada", [128, 96], F32)
        nw = calloc("nw", [128, 48], F32)
        pscale = calloc("pscale", [128, 8], F32)
        misc = calloc("misc", [128, 68], F32)
        modc = calloc("modc", [128, 6, 16], F32)
        modx = calloc("modx", [128, 6, 16], F32)
        Avec = calloc("Avec", [128, 3, 16], F32)
        masks = calloc("masks", [128, 4, 512], BF16)
        sink16 = calloc("sink16", [1, 16], F32)
        esink16 = calloc("esink16", [1, 16], F32)
        selz = calloc("selz", [128, 2, 128], BF16)
        m_sb = calloc("m_sb", [2, 512], F32)
        hU = calloc("hU", [128, 16, 16], BF16)
        pw = calloc("pw", [128, 4, 2, 256], BF16)
        wring = [calloc("wring%d" % i, [128, 16, 512], BF16) for i in range(3)]
        DYN = pos[0]
        assert DYN + 144 * KB <= ARENA_WORDS * 4, DYN
        dv = lambda name, off, shape, dtype: mkbuf(name, DYN + int(off), shape, dtype)

        banks = [es.enter_context(nc.psum_tensor("pb%d" % i, [128, 512], F32)) for i in range(8)]
        bres = [Res() for _ in range(8)]
        wsem = [K.dsem("dw%d" % i) for i in range(3)]
        csem = K.dsem("dconst")
        stsem = [K.dsem("dst%d" % i) for i in range(4)]
        osem = [K.dsem("do%d" % i) for i in range(4)]

        cbufs = (ident, cvec, bada, nw, pscale, misc, sink16, masks)
        for b, src in zip(cbufs, (ident_d, cvec_d, bada_d, nw_d, pscale_d, misc_d, sink_d, masks_d)):
            K.dma("sp", csem, b.ap, src, w=[b.r()])
        cev = Ev(csem["sem"], csem["cnt"], None)
        for b in cbufs:
            b.r().w = cev
        K.dma("pool", K.dsem("dpw"), pw.ap, pool_w.rearrange("g (k p) n -> p g k n", p=128), w=[pw.r()])
        K.op("dve", lambda e: e.memset(ones_bf.ap, 1.0), w=[ones_bf.r()])
        K.op("dve", lambda e: e.memset(selz.ap, 0.0), w=[selz.r()])
        K.op("dve", lambda e: e.memset(selz.ap[0:1, 0, 64:128], 1.0), w=[selz.r()])
        K.op("dve", lambda e: e.memset(selz.ap[0:1, 1, 0:64], 1.0), w=[selz.r()])
        K.op("act", lambda e: e.activation(out=cs.ap, in_=cvec.ap.rearrange("p (k t) -> p k t", t=2), func=AF.Silu),
             r=[cvec.r()], w=[cs.r()])

        pieces = [("ada", j) for q in range(4) for j in (q, 4 + q)]
        pieces += [("in", 2), ("ada", 8), ("in", 0), ("ada", 9), ("in", 1), ("ada", 10), ("ada", 11)]
        ATT_SCHED = att_schedule()
        for act in ATT_SCHED:
            if act[0] == "piece":
                pieces.append((act[1], act[2]))
        for j in range(4):
            pieces += [("ada", 20 + j), ("out", j)]
        for p in range(4):
            pieces += [("up", p * 4 + j) for j in range(4)]
            pieces += [("dn", p * 4 + j) for j in range(4)]

        def piece_src(kind, j):
            if kind == "ada":
                m = w_ada[:, j * 512:(j + 1) * 512]
            elif kind == "in":
                m = w_in[:, j * 512:(j + 1) * 512]
            elif kind == "out":
                m = w_out[:, j * 512:(j + 1) * 512]
            elif kind == "up":
                m = w_up[:, j * 512:(j + 1) * 512]
            else:
                p, jj = divmod(j, 4)
                m = w_dn[p * 2048:(p + 1) * 2048, jj * 512:(jj + 1) * 512]
            return m.rearrange("(k p) n -> p k n", p=128)

        state = {"issued": 0, "cur": -1, "active": None, "last": None}
        free_slots = [0, 1, 2]
        slot_of = {}

        def prefetch():
            while free_slots and state["issued"] < len(pieces) and state["issued"] <= state["cur"] + 2:
                i = state["issued"]
                s = free_slots.pop(0)
                slot_of[i] = s
                K.dma("pool", wsem[s], wring[s].ap, piece_src(*pieces[i]), w=[wring[s].r()])
                state["issued"] += 1

        def next_piece(kind, j, pin=False):
            if state["active"] is not None:
                free_slots.append(state["active"])
                state["active"] = None
            state["cur"] += 1
            i = state["cur"]
            assert pieces[i] == (kind, j), (pieces[i], kind, j)
            if i not in slot_of:
                prefetch()
            s = slot_of[i]
            if not pin:
                state["active"] = s
            state["last"] = s
            prefetch()
            return wring[s].ap, wring[s].r()

        def unpin(s):
            free_slots.append(s)
            prefetch()

        prefetch()
        bank_i = [0]

        def nb():
            b = bank_i[0] % 8
            bank_i[0] += 1
            return b

        def ada_piece(j):
            ada_flush()
            W, wr = next_piece("ada", j)
            b = nb()
            for r4 in range(4):
                for jc in range(4):
                    kc = r4 * 4 + jc
                    K.op("pe", lambda e: e.matmul(banks[b][32 * jc:32 * jc + 2, :], lhsT=cs.ap[:, kc, :], rhs=W[:, kc, :],
                                                  start=(r4 == 0), stop=(r4 == 3), tile_position=(0, 32 * jc)),
                         r=[wr, cs.r()], w=[bres[b]], inc=(kc == 15))
            K.op("act", lambda e: e.activation(out=m_sb.ap, in_=banks[b][0:2, :], func=AF.Identity),
                 x=[bres[b]], w=[m_sb.r()])
            for jc in range(1, 4):
                K.op("dve", lambda e: e.tensor_tensor(out=m_sb.ap, in0=banks[b][32 * jc:32 * jc + 2, :], in1=m_sb.ap, op=ALU.add),
                     r=[m_sb.r()], x=[bres[b]], w=[m_sb.r()])
            ada_pend.append(j)

        ada_pend = []

        def ada_flush():
            while ada_pend:
                j = ada_pend.pop(0)
                s, kq = j // 4, (j % 4) * 4
                b2 = nb()
                for c in range(4):
                    K.op("pe", lambda e: e.transpose(banks[b2][:, 2 * c:2 * c + 2], m_sb.ap[0:2, c * 128:(c + 1) * 128],
                                                     ident.ap[0:2, 0:2]),
                         r=[m_sb.r(), ident.r()], w=[bres[b2]], inc=(c == 3))
                pv = banks[b2][:, 0:8].rearrange("p (c t) -> p c t", t=2)
                for t, mm in ((0, modc), (1, modx)):
                    K.op("dve", lambda e: e.tensor_tensor(out=mm.ap[:, s, kq:kq + 4], in0=pv[:, :, t],
                                                          in1=bada.ap[:, s * 16 + kq:s * 16 + kq + 4], op=ALU.add),
                         r=[bada.r()], x=[bres[b2]], w=[mm.r(s, kq)])

        def mod_r(mm, s):
            return [mm.r(s, kq) for kq in (0, 4, 8, 12)]

        def make_A(idx, mm, nwoff, kqs=(0, 4, 8, 12)):
            s = 1 if idx < 2 else 4
            for kq in kqs:
                K.op("dve", lambda e: e.scalar_tensor_tensor(out=Avec.ap[:, idx, kq:kq + 4], in0=mm.ap[:, s, kq:kq + 4], scalar=1.0,
                                                             in1=nw.ap[:, nwoff + kq:nwoff + kq + 4], op0=ALU.add, op1=ALU.mult),
                     r=[mm.r(s, kq), nw.r()], w=[Avec.r(idx, kq)])

        hTg = [dv("hT%d" % g, 16 * KB * g, [128, 16, 512], BF16) for g in range(3)]
        stage = [dv("stage0", 48 * KB, [128, 2048], F32), dv("stage1", 56 * KB, [128, 2048], F32)]
        xTg = [dv("xTg0", 64 * KB, [128, 16, 512], F32), dv("xTg1", 96 * KB, [128, 16, 512], F32)]
        sq = [dv("sq%d" % i, (128 + i) * KB, [128, 512], BF16) for i in range(4)]
        lnb = dv("lnb", 132 * KB, [128, 512], F32)
        rstd = [dv("rstd0", 134 * KB, [128, 512], F32), dv("rstd1", 136 * KB, [128, 512], F32)]
        tmpA = [dv("tmpA0", 138 * KB, [128, 512], F32), dv("tmpA1", 140 * KB, [128, 512], F32)]
        xall_v = xall.rearrange("(b p) f -> p b f", p=128)
        sq_i = [0]

        def hT_r(g, c, c0, c1):
            return [hTg[g].r(c, bb) for bb in range(c0 // 128, (c1 + 127) // 128)]

        def load_transpose(dst, dst_r, blk, t, stg):
            h = t % len(stg)
            K.dma("sp", stsem[h], stg[h].ap, xall_v[:, blk, :], w=[stg[h].r()])
            for cq in range(4):
                b = nb()
                for ci in range(4):
                    c = cq * 4 + ci
                    K.op("pe", lambda e: e.transpose(banks[b][:, ci * 128:(ci + 1) * 128],
                                                     stg[h].ap[:, c * 128:(c + 1) * 128], ident.ap),
                         r=[stg[h].r(), ident.r()], w=[bres[b]], inc=(ci == 3))
                K.op("act", lambda e: e.activation(out=dst[:, cq * 4:(cq + 1) * 4, t * 128:(t + 1) * 128],
                                                   in_=banks[b][:, :].rearrange("p (c t) -> p c t", c=4), func=AF.Identity),
                     x=[bres[b]], w=dst_r(cq, t))

        def stats_rstd(src, src_r, out_b):
            statb = nb()
            for c in range(16):
                si = sq_i[0] % 4
                sq_i[0] += 1
                K.op("act", lambda e: e.activation(out=sq[si].ap, in_=src(c), func=AF.Square), r=src_r(c), w=[sq[si].r()])
                K.op("pe", lambda e: e.matmul(banks[statb][:, :], lhsT=ones_bf.ap, rhs=sq[si].ap, start=(c == 0), stop=(c == 15)),
                     r=[sq[si].r(), ones_bf.r()], w=[bres[statb]], inc=True)
            K.op("act", lambda e: e.activation(out=lnb.ap, in_=banks[statb][:, :], func=AF.Ln, scale=1.0 / D, bias=EPS),
                 x=[bres[statb]], w=[lnb.r()])
            K.op("act", lambda e: e.activation(out=out_b.ap, in_=lnb.ap, func=AF.Exp, scale=-0.5), r=[lnb.r()], w=[out_b.r()])

        def group_front(g, xb):
            for t in range(4):
                load_transpose(xTg[xb].ap, lambda cq, t: [xTg[xb].r(cq, t)], 4 * g + t, t, stage)
            stats_rstd(lambda c: xTg[xb].ap[:, c, :], lambda c: [xTg[xb].r(c // 4, t) for t in range(4)], rstd[xb])

        def group_h(g, xb, chunks=range(16)):
            for c in chunks:
                segs = [(0, 512, 0, modc)] if g < 2 else [(0, 256, 0, modc), (256, 512, 1, modx)]
                for si, (c0, c1, ai, mm) in enumerate(segs):
                    ti = (c + si) % 2
                    K.op("dve", lambda e: e.scalar_tensor_tensor(
                        out=tmpA[ti].ap[:, c0:c1], in0=xTg[xb].ap[:, c, c0:c1], scalar=Avec.ap[:, ai, c:c + 1],
                        in1=rstd[xb].ap[:, c0:c1], op0=ALU.mult, op1=ALU.mult),
                         r=[xTg[xb].r(c // 4, t) for t in range(c0 // 128, c1 // 128)] + [rstd[xb].r(), Avec.r(ai, (c // 4) * 4)], w=[tmpA[ti].r()])
                    K.op("act", lambda e: e.activation(
                        out=hTg[g].ap[:, c, c0:c1], in_=tmpA[ti].ap[:, c0:c1], func=AF.Identity,
                        bias=mm.ap[:, 0, c:c + 1]), r=[tmpA[ti].r(), mm.r(0, (c // 4) * 4)], w=hT_r(g, c, c0, c1))

        group_front(2, 0)
        group_front(0, 1)
        for q in range(4):
            ada_piece(q)
            ada_piece(4 + q)
            ada_flush()
            make_A(0, modc, 0, kqs=(4 * q,))
            make_A(1, modx, 0, kqs=(4 * q,))
            group_h(2, 0, range(4 * q, 4 * q + 4))
            group_h(0, 1, range(4 * q, 4 * q + 4))
        group_front(1, 0)
        group_h(1, 0)

        qT = dv("qT", 48 * KB, [128, 8, TOK], BF16)
        kTz = dv("kTz", 64 * KB, [128, 4, NT], BF16)
        vsb = dv("vsb", 76 * KB, [128, 12, 2, 2, 128], BF16)
        t1 = [dv("t1_0", 105 * KB, [128, 512], F32), dv("t1_1", 107 * KB, [128, 512], F32)]
        t2 = [dv("t2_0", 109 * KB, [128, 512], F32), dv("t2_1", 111 * KB, [128, 512], F32)]
        rope = dv("rope", 113 * KB, [128, 2, 1280], F32)
        ksum = [dv("ksum0", 123 * KB, [128, 512], F32), dv("ksum1", 125 * KB, [128, 512], F32)]
        mixP = dv("mixP", 128 * KB, [128, 8, TOK], BF16)
        K.dma("sp", K.dsem("drope"), rope.ap, rope_d, w=[rope.r()])
        K.op("pool", lambda e: e.memset(vsb.flat, 1.0), w=[vsb.r()])
        vsb_all = vsb.r()
        ks_i = [0]
        rope_i = [0]

        def rope_evac(b, cols):
            i = rope_i[0] % 2
            rope_i[0] += 1
            n = cols.stop - cols.start
            K.op("dve", lambda e: e.tensor_tensor(out=t1[i].ap[:, 0:n], in0=banks[b][:, 0:n], in1=rope.ap[:, 0, cols], op=ALU.mult),
                 r=[rope.r()], x=[bres[b]], w=[t1[i].r()])
            for hf in range(2):
                po = (1 - hf) * 64
                K.op("dve", lambda e: e.tensor_tensor(out=t2[i].ap[hf * 64:(hf + 1) * 64, 0:n], in0=banks[b][po:po + 64, 0:n],
                                                      in1=rope.ap[hf * 64:(hf + 1) * 64, 1, cols], op=ALU.mult),
                     r=[rope.r()], x=[bres[b]], w=[t2[i].r(hf)])
            return i

        def proj_g(W, wr, ocs, g, c0, c1, b):
            for kc in range(16):
                K.op("pe", lambda e: e.matmul(banks[b][:, 0:c1 - c0], lhsT=W[:, kc, ocs], rhs=hTg[g].ap[:, kc, c0:c1],
                                              start=(kc == 0), stop=(kc == 15)),
                     r=[wr] + hT_r(g, kc, c0, c1), w=[bres[b]], inc=(kc == 15))

        W, wr = next_piece("in", 2)
        for oc in range(2):
            ocs = slice(oc * 128, (oc + 1) * 128)
            ab = {}
            for g in (2, 0, 1):
                ab[g] = nb()
                proj_g(W, wr, ocs, g, 0, 512, ab[g])
            for g, (c0, c1) in ((2, (1024, 1280)), (0, (0, 512)), (1, (512, 1024))):
                i = rope_evac(ab[g], slice(c0, c1))
                n = c1 - c0
                ki = ks_i[0] % 2
                ks_i[0] += 1
                K.op("pool", lambda e: e.tensor_tensor(out=ksum[ki].ap[:, 0:n], in0=t1[i].ap[:, 0:n], in1=t2[i].ap[:, 0:n], op=ALU.add),
                     r=[t1[i].r(), t2[i].r(0), t2[i].r(1)], w=[ksum[ki].r()])
                for b2 in range(2):
                    hh = 2 * oc + b2
                    K.op("pool", lambda e: e.tensor_scalar(out=kTz.ap[:, hh, c0:c1], in0=ksum[ki].ap[:, 0:n], scalar1=misc.ap[:, 66 + b2:67 + b2],
                                                           scalar2=1.0, op0=ALU.mult, op1=ALU.mult),
                         r=[ksum[ki].r(), misc.r()], w=[kTz.r(hh, g)])
            for b2 in range(2):
                hh = 2 * oc + b2
                K.op("act", lambda e: e.activation(out=kTz.ap[:, hh, 1280:1536], in_=banks[ab[2]][:, 256:512], func=AF.Identity,
                                                   scale=misc.ap[:, 66 + b2:67 + b2]),
                     r=[misc.r()], x=[bres[ab[2]]], w=[kTz.r(hh, 3)])

        def kTz_r(hh, kb):
            gi = 0 if kb < 4 else 1 if kb < 8 else 2 if kb < 10 else 3
            return [kTz.r(hh, gi)]

        for tb in [8, 9, 10, 11, 0, 1, 2, 3, 4, 5, 6, 7]:
            b = nb()
            g, tl = tb // 4, tb % 4
            for kc in range(16):
                K.op("pe", lambda e: e.matmul(banks[b][:, 0:256], lhsT=hTg[g].ap[:, kc, tl * 128:(tl + 1) * 128],
                                              rhs=W[:, kc, 256:512], start=(kc == 0), stop=(kc == 15)),
                     r=[wr, hTg[g].r(kc, tl)], w=[bres[b]], inc=(kc == 15))
            pvv = banks[b][:, 0:256].rearrange("p (a b d) -> p a b d", a=2, b=2)
            for par in range(2):
                K.op("act", lambda e: e.activation(out=vsb.ap[:, tb, :, par, par * 64:(par + 1) * 64], in_=pvv[:, :, par, :], func=AF.Identity),
                     r=[vsb_all], x=[bres[b]], w=[vsb.r(tb, par)])
        K.op("pool", lambda e: e.tensor_copy(out=hU.ap, in_=hTg[2].ap[:, :, 120:136]),
             r=[hTg[2].r(c, bb) for c in range(16) for bb in (0, 1)], w=[hU.r()])
        ada_piece(8)

        for qp in range(2):
            W, wr = next_piece("in", qp)
            for oc in range(4):
                ch = qp * 4 + oc
                ocs = slice(oc * 128, (oc + 1) * 128)
                for g in range(2):
                    b = nb()
                    proj_g(W, wr, ocs, g, 0, 512, b)
                    i = rope_evac(b, slice(g * 512, (g + 1) * 512))
                    K.op("pool", lambda e: e.tensor_tensor(out=qT.ap[:, ch, g * 512:(g + 1) * 512], in0=t1[i].ap, in1=t2[i].ap, op=ALU.add),
                         r=[t1[i].r(), t2[i].r(0), t2[i].r(1)], w=[qT.r(ch, g)])
            ada_piece(9 + qp)
        ada_piece(11)

        mixA = dv("mixA", 32 * KB, [128, 8, TOK], BF16)
        pooledT = dv("pooledT", 88 * KB, [128, 2, TOK], BF16)
        Ub = [dv("U0", 92 * KB, [128, 1040], F32), dv("U1", 123 * KB, [128, 1040], F32)]
        scr = [dv("scrA", 92 * KB + 4160, [128, 1040], F32), dv("scrB", 92 * KB + 8320, [128, 1040], F32)]
        PT = [dv("PT%d" % i, (105 + i) * KB, [128, 512], BF16) for i in range(10)]
        rec = [dv("rec0", 115 * KB, [128, 512], F32), dv("rec1", 117 * KB, [128, 512], F32)]
        esinkz = dv("esinkz", 119 * KB, [128, 16, 128], BF16)
        K.op("act", lambda e: e.activation(out=esink16.ap, in_=sink16.ap, func=AF.Exp), r=[sink16.r()], w=[esink16.r()])
        K.op("dve", lambda e: e.memset(esinkz.flat, 0.0), w=[esinkz.r()])
        K.op("dve", lambda e: e.tensor_copy(out=esinkz.ap[0:1, :, :], in_=esink16.ap.unsqueeze(2).broadcast_to([1, 16, 128])),
             r=[esink16.r()], w=[esinkz.r()])
        stb, otb, mib = [0, 1, 2, 3], [4, 5], [6, 7]
        st_i, pt_i, ot_i, mi_i = [0], [0], [0], [0]

        def nbm():
            b = mib[mi_i[0] % 2]
            mi_i[0] += 1
            return b

        def key_blocks(n):
            return [(n - 1 if n > 0 else 8, 0 if n == 0 else 1), (n, None), (n + 1 if n < 7 else 9, 3 if n == 7 else 2),
                    (10, None), (11, None)]

        def emit_st(n, h):
            c0 = 4 * (h // 2)
            tiles = []
            for (kb, mk) in key_blocks(n):
                b = stb[st_i[0] % 4]
                st_i[0] += 1
                K.op("pe", lambda e: e.matmul(banks[b][:, :], lhsT=kTz.ap[:, h, kb * 128:(kb + 1) * 128],
                                              rhs=qT.ap[:, c0:c0 + 4, n * 128:(n + 1) * 128], start=True, stop=True),
                     r=kTz_r(h, kb) + [qT.r(c0 + g, n // 4) for g in range(4)], w=[bres[b]])
                p = pt_i[0] % 10
                pt_i[0] += 1
                K.op("act", lambda e: e.activation(out=PT[p].ap, in_=banks[b][:, :], func=AF.Exp, scale=0.125),
                     x=[bres[b]], w=[PT[p].r()])
                if mk is not None:
                    K.op("dve", lambda e: e.tensor_tensor(out=PT[p].ap, in0=PT[p].ap, in1=masks.ap[:, mk, :], op=ALU.mult),
                         r=[masks.r()], w=[PT[p].r()])
                tiles.append((kb, p))
            return tiles

        def emit_pv(n, h, tiles):
            hb = (h % 2) * 64
            c0 = 4 * (h // 2)
            b = otb[ot_i[0] % 2]
            ri = ot_i[0] % 2
            ot_i[0] += 1
            K.op("pe", lambda e: e.matmul(banks[b][:, :], lhsT=selz.ap[:, h % 2, :], rhs=esinkz.ap[:, h * 4:(h + 1) * 4, :], start=True, stop=False),
                 r=[selz.r(), esinkz.r()], w=[bres[b]], inc=False)
            for ti, (kb, p) in enumerate(tiles):
                K.op("pe", lambda e: e.matmul(banks[b][:, :], lhsT=vsb.ap[:, kb, h // 2, h % 2, :], rhs=PT[p].ap, start=False, stop=(ti == 4)),
                     r=[vsb.r(kb, 0), vsb.r(kb, 1), PT[p].r()], w=[bres[b]], inc=(ti == 4))
            ob, db = hb, 64 - hb
            K.op("act", lambda e: e.activation(out=rec[ri].ap[db:db + 64, :], in_=banks[b][db:db + 64, :], func=AF.Ln),
                 x=[bres[b]], w=[rec[ri].r()])
            K.op("act", lambda e: e.activation(out=rec[ri].ap[db:db + 64, :], in_=rec[ri].ap[db:db + 64, :], func=AF.Exp, scale=-1.0),
                 r=[rec[ri].r()], w=[rec[ri].r()])
            K.op("dve", lambda e: e.tensor_tensor(out=mixA.ap[ob:ob + 64, c0:c0 + 4, n * 128:(n + 1) * 128],
                                                  in0=banks[b][ob:ob + 64, :].rearrange("p (g q) -> p g q", g=4),
                                                  in1=rec[ri].ap[db:db + 64, :].rearrange("p (g q) -> p g q", g=4), op=ALU.mult),
                 r=[rec[ri].r()], x=[bres[b]], w=[mixA.r(h, n)])

        hflag = misc.ap[:, 0:2]
        invc = misc.ap[:, 2:66].rearrange("p (g t) -> p g t", g=4)
        ust = {}

        def u_proj(up, oc, g):
            W, wr = ust["W"]
            uc = up * 4 + oc
            U = Ub[uc % 2]
            ocs = slice(oc * 128, (oc + 1) * 128)
            b = nbm()
            if g < 2:
                proj_g(W, wr, ocs, g, 0, 512, b)
                K.op("act", lambda e: e.activation(out=U.ap[:, 8 + g * 512:8 + (g + 1) * 512], in_=banks[b][:, :], func=AF.Identity),
                     x=[bres[b]], w=[U.r(g)])
            else:
                for kc in range(16):
                    K.op("pe", lambda e: e.matmul(banks[b][:, 0:16], lhsT=W[:, kc, ocs], rhs=hU.ap[:, kc, :], start=(kc == 0), stop=(kc == 15)),
                         r=[wr, hU.r()], w=[bres[b]], inc=(kc == 15))
                K.op("act", lambda e: e.activation(out=U.ap[:, 0:8], in_=banks[b][:, 0:8], func=AF.Identity, scale=hflag[:, 0:1]),
                     r=[misc.r()], x=[bres[b]], w=[U.r(2)])
                K.op("act", lambda e: e.activation(out=U.ap[:, 1032:1040], in_=banks[b][:, 8:16], func=AF.Identity, scale=hflag[:, 1:2]),
                     r=[misc.r()], x=[bres[b]], w=[U.r(3)])

        pst = {}

        def u_pool(up, oc, part):
            uc = up * 4 + oc
            gi = uc // 2
            wsz = WINS[gi]
            U = Ub[uc % 2]
            Ur = [U.r(i) for i in range(4)]
            if part == 0:
                K.op("dve", lambda e: e.tensor_tensor(out=scr[0].ap[:, 0:1039], in0=U.ap[:, 0:1039], in1=U.ap[:, 1:1040], op=ALU.add),
                     r=Ur, w=[scr[0].r()])
                pst.update(ci=0, ln=1039, step=2)
                nsteps = 1
            elif part == 1:
                nsteps = 2
            else:
                nsteps = 0
            if part < 2:
                while pst["step"] < wsz and nsteps > 0:
                    ci, ln, step = pst["ci"], pst["ln"], pst["step"]
                    nl = ln - step
                    K.op("dve", lambda e: e.tensor_tensor(out=scr[1 - ci].ap[:, 0:nl], in0=scr[ci].ap[:, 0:nl],
                                                          in1=scr[ci].ap[:, step:step + nl], op=ALU.add),
                         r=[scr[ci].r()], w=[scr[1 - ci].r()])
                    pst.update(ci=1 - ci, ln=nl, step=step * 2)
                    nsteps -= 1
                return
            ci = pst["ci"]
            assert pst["step"] >= wsz
            off = 8 - wsz // 2
            ps = oc % 2
            K.op("dve", lambda e: e.scalar_tensor_tensor(out=pooledT.ap[:, ps, :], in0=scr[ci].ap[:, off:off + 1024], scalar=1.0 / wsz,
                                                         in1=U.ap[:, 8:1032], op0=ALU.mult, op1=ALU.subtract),
                 r=[scr[ci].r()] + Ur, w=[pooledT.r(ps)])
            for (ta, ia) in ((0, 0), (1016, 8)):
                K.op("dve", lambda e: e.tensor_tensor(out=scr[1 - ci].ap[:, 0:8], in0=scr[ci].ap[:, off + ta:off + ta + 8],
                                                      in1=invc[:, gi, ia:ia + 8], op=ALU.mult),
                     r=[scr[ci].r(), misc.r()], w=[scr[1 - ci].r()])
                K.op("dve", lambda e: e.tensor_tensor(out=pooledT.ap[:, ps, ta:ta + 8], in0=scr[1 - ci].ap[:, 0:8],
                                                      in1=U.ap[:, 8 + ta:16 + ta], op=ALU.subtract),
                     r=[scr[1 - ci].r()] + Ur, w=[pooledT.r(ps)])

        def u_poolw(up, oc):
            gi = (up * 4 + oc) // 2
            for po in range(2):
                for g2 in range(2):
                    b = nbm()
                    for k2 in range(2):
                        K.op("pe", lambda e: e.matmul(banks[b][:, :], lhsT=pw.ap[:, gi, k2, po * 128:(po + 1) * 128],
                                                      rhs=pooledT.ap[:, k2, g2 * 512:(g2 + 1) * 512], start=(k2 == 0), stop=(k2 == 1)),
                             r=[pw.r(), pooledT.r(k2)], w=[bres[b]], inc=(k2 == 1))
                    K.op("act", lambda e: e.activation(out=mixP.ap[:, 2 * gi + po, g2 * 512:(g2 + 1) * 512], in_=banks[b][:, :],
                                                       func=AF.Identity, scale=pscale.ap[:, 2 * gi + po:2 * gi + po + 1]),
                         r=[pscale.r()], x=[bres[b]], w=[mixP.r(2 * gi + po, g2)])

        nb_save = nb
        nb = nbm
        prev = None
        for act in ATT_SCHED:
            if act[0] == "att":
                n, h = divmod(act[1], 4)
                tiles = emit_st(n, h)
                if prev is not None:
                    emit_pv(*prev)
                prev = (n, h, tiles)
            elif act[0] == "piece":
                if act[1] == "ada":
                    ada_piece(act[2])
                else:
                    if "slot" in ust:
                        unpin(ust["slot"])
                    ust["W"] = next_piece("in", act[2], pin=True)
                    ust["slot"] = state["last"]
            elif act[0] == "uproj":
                u_proj(*act[1:])
            elif act[0] == "upool":
                u_pool(*act[1:])
            elif act[0] == "poolw":
                u_poolw(*act[1:])
        emit_pv(*prev)
        unpin(ust["slot"])
        nb = nb_save

        stage2 = [dv("stage%db" % i, 8 * i * KB, [128, 2048], F32) for i in range(4)]
        x1T = dv("x1T", 64 * KB, [128, 16, TOK], F32)
        sq = [dv("sq%db" % i, (48 + i) * KB, [128, 512], BF16) for i in range(4)]
        lnb = dv("lnbb", 52 * KB, [128, 512], F32)
        rstd = [dv("rstd0b", 54 * KB, [128, 512], F32), dv("rstd1b", 56 * KB, [128, 512], F32)]
        tmpA = [dv("tmpA0b", 58 * KB, [128, 512], F32), dv("tmpA1b", 60 * KB, [128, 512], F32)]
        for t in range(8):
            load_transpose(x1T.ap, lambda cq, t: [x1T.r(c, t) for c in range(cq * 4, cq * 4 + 4)], t, t, stage2)

        def mix_rhs(kc, g2):
            if kc < 8:
                return (mixA.ap[:, kc, g2 * 512:(g2 + 1) * 512],
                        [mixA.r(2 * (kc // 4) + hh, n) for hh in range(2) for n in range(4 * g2, 4 * g2 + 4)])
            return mixP.ap[:, kc - 8, g2 * 512:(g2 + 1) * 512], [mixP.r(kc - 8, g2)]

        def x1_r(c, g2):
            return [x1T.r(c, t) for t in range(4 * g2, 4 * g2 + 4)]

        ring6 = [0]
        pend = []

        def nb6():
            b = ring6[0] % 6
            ring6[0] += 1
            return b

        def flush_stats():
            while pend:
                OC, g2, si = pend.pop(0)
                K.op("pe", lambda e: e.matmul(banks[6 + g2][:, :], lhsT=ones_bf.ap, rhs=sq[si].ap, start=(OC == 0), stop=(OC == 15)),
                     r=[sq[si].r(), ones_bf.r()], w=[bres[6 + g2]], inc=True)
                if OC == 15:
                    K.op("act", lambda e: e.activation(out=lnb.ap, in_=banks[6 + g2][:, :], func=AF.Ln, scale=1.0 / D, bias=EPS),
                         x=[bres[6 + g2]], w=[lnb.r()])
                    K.op("act", lambda e: e.activation(out=rstd[g2].ap, in_=lnb.ap, func=AF.Exp, scale=-0.5), r=[lnb.r()], w=[rstd[g2].r()])

        def proj2_g(W, wr, oc, rhs_fn, OC, gate_s, g2, stats=False, scale_fw=False):
            b = nb6() if stats else nb()
            for kc in range(16):
                rhs, rr = rhs_fn(kc, g2)
                K.op("pe", lambda e: e.matmul(banks[b][:, :], lhsT=W[:, kc, oc * 128:(oc + 1) * 128], rhs=rhs,
                                              start=(kc == 0), stop=(kc == 15)),
                     r=[wr] + rr, w=[bres[b]], inc=(kc == 15))
            flush_stats()
            K.op("dve", lambda e: e.scalar_tensor_tensor(
                out=x1T.ap[:, OC, g2 * 512:(g2 + 1) * 512], in0=banks[b][:, :], scalar=modc.ap[:, gate_s, OC:OC + 1],
                in1=x1T.ap[:, OC, g2 * 512:(g2 + 1) * 512], op0=ALU.mult, op1=ALU.add),
                 r=x1_r(OC, g2) + [modc.r(gate_s, (OC // 4) * 4)], x=[bres[b]], w=x1_r(OC, g2))
            if stats:
                si = sq_i[0] % 4
                sq_i[0] += 1
                K.op("act", lambda e: e.activation(out=sq[si].ap, in_=x1T.ap[:, OC, g2 * 512:(g2 + 1) * 512], func=AF.Square),
                     r=x1_r(OC, g2), w=[sq[si].r()])
                pend.append((OC, g2, si))
                if scale_fw:
                    K.op("dve", lambda e: e.tensor_scalar(out=x1T.ap[:, OC, g2 * 512:(g2 + 1) * 512], in0=x1T.ap[:, OC, g2 * 512:(g2 + 1) * 512],
                                                          scalar1=nw.ap[:, 32 + OC:33 + OC], scalar2=None, op0=ALU.mult),
                         r=[nw.r()], w=x1_r(OC, g2))

        def proj_piece(W, wr, j, rhs_fn, gate_s, stats=False, scale_fw=False, group_major=False):
            if group_major:
                for g2 in range(2):
                    for oc in range(4):
                        proj2_g(W, wr, oc, rhs_fn, j * 4 + oc, gate_s, g2, stats, scale_fw)
            else:
                for oc in range(4):
                    for g2 in range(2):
                        proj2_g(W, wr, oc, rhs_fn, j * 4 + oc, gate_s, g2, stats, scale_fw)

        nb_save = nb
        nb = nb6
        for j in range(4):
            ada_piece(20 + j)
            W, wr = next_piece("out", j)
            proj_piece(W, wr, j, mix_rhs, 2, stats=True, group_major=(j == 3))
        flush_stats()
        ada_flush()
        nb = nb_save
        make_A(2, modc, 16)

        hmT = dv("hmT", 16 * KB, [128, 16, TOK], BF16)
        for g2 in range(2):
            for c in range(16):
                ti = c % 2
                K.op("dve", lambda e: e.scalar_tensor_tensor(
                    out=tmpA[ti].ap, in0=x1T.ap[:, c, g2 * 512:(g2 + 1) * 512], scalar=Avec.ap[:, 2, c:c + 1], in1=rstd[g2].ap,
                    op0=ALU.mult, op1=ALU.mult), r=x1_r(c, g2) + [rstd[g2].r(), Avec.r(2, (c // 4) * 4)], w=[tmpA[ti].r()])
                K.op("act", lambda e: e.activation(out=hmT.ap[:, c, g2 * 512:(g2 + 1) * 512], in_=tmpA[ti].ap, func=AF.Identity,
                                                   bias=modc.ap[:, 3, c:c + 1]), r=[tmpA[ti].r()] + mod_r(modc, 3), w=[hmT.r(c, g2)])
        aTh = [dv("aT_lo", 48 * KB, [128, 8, TOK], BF16), dv("aT_hi", 128 * KB, [128, 8, TOK], BF16)]
        rl = [dv("rl0", 0, [128, 512], F32), dv("rl1", 2 * KB, [128, 512], F32)]
        sq = [dv("sq%dc" % i, (4 + i) * KB, [128, 512], BF16) for i in range(4)]
        lnb = dv("lnbc", 8 * KB, [128, 512], F32)
        rstd = [dv("rstd0c", 10 * KB, [128, 512], F32), dv("rstd1c", 12 * KB, [128, 512], F32)]
        rl_i = [0]

        def aT_rhs(kc, g2):
            return aTh[kc // 8].ap[:, kc % 8, g2 * 512:(g2 + 1) * 512], [aTh[kc // 8].r(kc % 8, g2)]

        for p in range(4):
            for j in range(4):
                W, wr = next_piece("up", p * 4 + j)
                for oc in range(4):
                    fc = j * 4 + oc
                    ab = [nb6(), nb6()] if p == 3 else [nb(), nb()]
                    for g2 in range(2):
                        for kc in range(16):
                            b = ab[g2]
                            K.op("pe", lambda e: e.matmul(banks[b][:, :], lhsT=W[:, kc, oc * 128:(oc + 1) * 128],
                                                          rhs=hmT.ap[:, kc, g2 * 512:(g2 + 1) * 512], start=(kc == 0), stop=(kc == 15)),
                                 r=[wr, hmT.r(kc, g2)], w=[bres[b]], inc=(kc == 15))
                    for g2 in range(2):
                        b = ab[g2]
                        ri = rl_i[0] % 2
                        rl_i[0] += 1
                        K.op("act", lambda e: e.activation(out=rl[ri].ap, in_=banks[b][:, :], func=AF.Relu), x=[bres[b]], w=[rl[ri].r()])
                        K.op("dve", lambda e: e.tensor_tensor(out=aTh[fc // 8].ap[:, fc % 8, g2 * 512:(g2 + 1) * 512], in0=rl[ri].ap, in1=rl[ri].ap, op=ALU.mult),
                             r=[rl[ri].r()], w=[aTh[fc // 8].r(fc % 8, g2)])
            for j in range(4):
                W, wr = next_piece("dn", p * 4 + j)
                proj_piece(W, wr, j, aT_rhs, 5, stats=(p == 3), scale_fw=(p == 3), group_major=(p == 3 and j == 3))
        flush_stats()

        ostage = [dv("ost%d" % i, (32 + 8 * i) * KB, [128, 2048], F32) for i in range(4)]
        rcol = dv("rcol", 16 * KB, [128, 8], F32)
        bc = nb()
        for t in range(8):
            g2, tt = divmod(t, 4)
            K.op("pe", lambda e: e.transpose(banks[bc][:, t:t + 1], rstd[g2].ap[0:1, tt * 128:(tt + 1) * 128], ident.ap[0:1, 0:1]),
                 r=[rstd[g2].r(), ident.r()], w=[bres[bc]], inc=(t == 7))
        K.op("dve", lambda e: e.tensor_copy(out=rcol.ap, in_=banks[bc][:, 0:8]), x=[bres[bc]], w=[rcol.r()])
        for t in range(8):
            yi = t % 4
            for cq in range(4):
                b = nb()
                for ci in range(4):
                    c = cq * 4 + ci
                    K.op("pe", lambda e: e.transpose(banks[b][:, ci * 128:(ci + 1) * 128], x1T.ap[:, c, t * 128:(t + 1) * 128], ident.ap),
                         r=[x1T.r(c, t), ident.r()], w=[bres[b]], inc=(ci == 3))
                eng = "act" if cq % 2 == 0 else "dve"
                if eng == "act":
                    K.op("act", lambda e: e.activation(out=ostage[yi].ap[:, cq * 512:(cq + 1) * 512], in_=banks[b][:, :], func=AF.Identity,
                                                       scale=rcol.ap[:, t:t + 1]), r=[rcol.r()], x=[bres[b]], w=[ostage[yi].r(cq)])
                else:
                    K.op("dve", lambda e: e.tensor_scalar(out=ostage[yi].ap[:, cq * 512:(cq + 1) * 512], in0=banks[b][:, :],
                                                          scalar1=rcol.ap[:, t:t + 1], scalar2=None, op0=ALU.mult),
                         r=[rcol.r()], x=[bres[b]], w=[ostage[yi].r(cq)])
            K.dma("sp", osem[yi], out_d[t * 128:(t + 1) * 128, :], ostage[yi].ap, r=[ostage[yi].r(cq) for cq in range(4)])
        for d in osem:
            nc.sync.wait_ge(d["sem"], d["cnt"])
        assert state["cur"] == len(pieces) - 1
        stuck, sems = K.check_deadlock()
        if stuck:
            names = {id(K.sem[k]): "s_" + k for k in K.ENG}
            raise RuntimeError("deadlock: " + str({k: (v[0], v[1], v[2][0], names.get(v[2][1], "dma"), v[2][2], sems.get(v[2][1], 0)) for k, v in stuck.items()}))
    return nc


def _prep(inputs):
    f = lambda a: np.ascontiguousarray(np.asarray(a, dtype=np.float32))
    x = f(inputs["x"])[0]
    ctx = f(inputs["ctx"])[0]
    c = f(inputs["c"])[0]
    c_ctx = f(inputs["c_ctx"])
    w_ada = f(inputs["w_ada"])[0]
    b_ada = f(inputs["b_ada"])[0]
    w_in = f(inputs["w_in"])[0]
    w_out = f(inputs["w_out"])[0]
    rmap = [_dmap(r) for r in range(128)]
    qcols = []
    for cch in range(8):
        for r in range(128):
            b, d, _, _ = rmap[r]
            hq = 4 * (2 * (cch // 4) + b) + cch % 4
            qcols.append(hq * 64 + d)
    kcols = []
    for kc in range(2):
        for r in range(128):
            b, d, _, _ = rmap[r]
            kcols.append(1024 + (2 * kc + b) * 64 + d)
    cols = np.concatenate([np.array(qcols), np.array(kcols), np.arange(1280, 2560)])
    w_in_p = np.ascontiguousarray(w_in[:, cols])
    orow = []
    for cch in range(8):
        for b in range(2):
            hq = 4 * (2 * (cch // 4) + b) + cch % 4
            orow.append(hq * 64 + np.arange(64))
    rows = np.concatenate(orow + [np.arange(1024, 2048)])
    w_out_p = np.ascontiguousarray(w_out[rows, :])
    pm = lambda v: np.ascontiguousarray(v.reshape(-1, 128).T)
    cvec = np.stack([pm(c), pm(c_ctx)], axis=2).reshape(128, 32)
    bada = pm(b_ada)
    nw = np.concatenate([pm(f(inputs["norm_attn_w"])[0]), pm(f(inputs["norm_mlp_w"])[0]), pm(f(inputs["final_norm_w"]))], axis=1)
    pscale = pm(f(inputs["pool_scale"])[0])
    sink16 = np.ascontiguousarray(f(inputs["attn_sink"])[0][None, :])
    ident = np.eye(128, dtype=np.float32)
    jj = np.arange(128)[:, None]
    ii = np.arange(128)[None, :]
    m_prev = np.tile((ii <= jj).astype(np.float32), (1, 4))
    m_next = np.tile((jj <= ii).astype(np.float32), (1, 4))
    zeros = np.zeros_like(m_prev)
    inv_freq = (10000.0 ** (-np.arange(0, 32, 2, dtype=np.float32) / 32)).astype(np.float32)
    jfreq = np.array([rmap[r][3] % 16 for r in range(128)])
    use_col = np.array([rmap[r][3] >= 16 for r in range(128)])
    sgn = np.array([-1.0 if rmap[r][2] == 0 else 1.0 for r in range(128)], dtype=np.float32)
    shared = dict(w_ada=w_ada, w_in=w_in_p, w_out=w_out_p, w_up=f(inputs["w_mlp_up"])[0], w_dn=f(inputs["w_mlp_down"])[0],
                  pool_w=f(inputs["pool_w"])[0], cvec=cvec, bada=bada, nw=nw, pscale=pscale, sink16=sink16, ident=ident)
    in_maps = []
    zblk = np.zeros((128, D), np.float32)
    for i in range(NCORES):
        t0 = i * TOK
        prev = x[t0 - 128:t0] if i > 0 else zblk
        nxt = x[t0 + TOK:t0 + TOK + 128] if i < NCORES - 1 else zblk
        xall = np.concatenate([x[t0:t0 + TOK], prev, nxt, ctx], axis=0)
        tpos = np.concatenate([np.arange(t0, t0 + TOK), np.arange(t0 - 128, t0), np.arange(t0 + TOK, t0 + TOK + 128)])
        tpos = np.clip(tpos, 0, L - 1)
        row = (tpos // 64).astype(np.float32)
        col = (tpos % 64).astype(np.float32)
        posm = np.where(use_col[:, None], col[None, :], row[None, :]).astype(np.float32)
        ang = (posm * inv_freq[jfreq][:, None]).astype(np.float32)
        rope = np.stack([np.cos(ang), sgn[:, None] * np.sin(ang)], axis=1).astype(np.float32)
        masks = np.stack([zeros if i == 0 else m_prev, m_prev, m_next, zeros if i == NCORES - 1 else m_next], axis=1)
        misc = np.zeros((128, 68), np.float32)
        misc[:, 66] = [1.0 if rmap[r][0] == 0 else 0.0 for r in range(128)]
        misc[:, 67] = [1.0 if rmap[r][0] == 1 else 0.0 for r in range(128)]
        misc[:, 0] = 0.0 if i == 0 else 1.0
        misc[:, 1] = 0.0 if i == NCORES - 1 else 1.0
        for gi, w in enumerate(WINS):
            for ti, tl in enumerate(list(range(8)) + list(range(1016, 1024))):
                tg = t0 + tl
                lo = min(max(tg - w // 2, 0), L)
                hi = min(max(tg - w // 2 + w, 0), L)
                misc[:, 2 + gi * 16 + ti] = 1.0 / float(hi - lo)
        m = dict(shared)
        m.update(xall=np.ascontiguousarray(xall), rope=rope, masks=masks.astype(ml_dtypes.bfloat16), misc=misc)
        in_maps.append(m)
    return in_maps


def kernel(**inputs):
    in_maps = _prep(inputs)
    nc = build_program()
    resu = run_bass_kernel_spmd(nc, in_maps, core_ids=list(range(NCORES)))
    outs = [np.asarray(r["out"], dtype=np.float32) for r in resu.results]
    return np.concatenate(outs, axis=0)[None, :, :]
```
